# Optimizing a Trainium2 kernel written in Bass

```python
import jax, jax.numpy as jnp
from jax import lax
import numpy as np

D_MODEL = 1024
BATCH = 8
SEQ = 8192
DEPTH = 4

N_META = 16
GRID_W = 64
HEAD_DIM = 64
N_Q_HEADS = 8
N_KV_HEADS = 2
GQA_GROUP = N_Q_HEADS // N_KV_HEADS
ATTN_WIDTH = N_Q_HEADS * HEAD_DIM
KV_WIDTH = N_KV_HEADS * HEAD_DIM
FOURIER_GROUPS = 8
FOURIER_GROUP_DIM = 64
FOURIER_WIDTH = FOURIER_GROUPS * FOURIER_GROUP_DIM
MIX_WIDTH = FOURIER_WIDTH + ATTN_WIDTH
IN_WIDTH = 2 * FOURIER_WIDTH + ATTN_WIDTH + 2 * KV_WIDTH + ATTN_WIDTH
SPLIT_POINTS = (FOURIER_WIDTH, 2 * FOURIER_WIDTH, 2 * FOURIER_WIDTH + ATTN_WIDTH,
                2 * FOURIER_WIDTH + ATTN_WIDTH + KV_WIDTH,
                2 * FOURIER_WIDTH + ATTN_WIDTH + 2 * KV_WIDTH)
Q_BLOCK = 128
ROPE_THETA = 10000.0
AXIS_ROT_DIM = HEAD_DIM // 2
AXIS_N_FREQ = AXIS_ROT_DIM // 2
EPS = 1e-6

kernel_name = "hymba_fnet_axial_gqa_encoder"


def rmsnorm(x, g):
    xf = x.astype(jnp.float32)
    y = xf * lax.rsqrt(jnp.mean(xf * xf, axis=-1, keepdims=True) + EPS)
    return (y * g.astype(jnp.float32)).astype(x.dtype)


def axial_rope_tables(n_tok):
    rows = n_tok // GRID_W
    j = jnp.arange(n_tok, dtype=jnp.int32)
    row = (j // GRID_W - rows // 2).astype(jnp.float32)
    col = (j % GRID_W - GRID_W // 2).astype(jnp.float32)
    freqs = 1.0 / (ROPE_THETA ** (jnp.arange(AXIS_N_FREQ, dtype=jnp.float32) * 2.0 / AXIS_ROT_DIM))
    ang = jnp.concatenate([row[:, None] * freqs, col[:, None] * freqs], axis=-1)
    ang = jnp.concatenate([jnp.zeros((N_META, HEAD_DIM // 2), jnp.float32), ang], axis=0)
    return jnp.cos(ang), jnp.sin(ang)


def apply_rope(x, cos, sin):
    b, h, l, d = x.shape
    xf = x.astype(jnp.float32).reshape(b, h, l, d // 2, 2)
    x0, x1 = xf[..., 0], xf[..., 1]
    out = jnp.stack([x0 * cos - x1 * sin, x0 * sin + x1 * cos], axis=-1)
    return out.reshape(b, h, l, d).astype(x.dtype)


def gqa_block(q_blk, k, v):
    s = jnp.einsum('bkgqd,bksd->bkgqs', q_blk, k).astype(jnp.float32) * (HEAD_DIM ** -0.5)
    p = jax.nn.softmax(s, axis=-1)
    return jnp.einsum('bkgqs,bksd->bkgqd', p.astype(v.dtype), v)


def fourier_branch(f_in, w_fmix, b_fmix):
    b, l, _ = f_in.shape
    f = f_in.reshape(b, l, FOURIER_GROUPS, FOURIER_GROUP_DIM).astype(jnp.float32)
    fr = jnp.real(jnp.fft.fftn(f, axes=(1, 3), norm='ortho')).astype(f_in.dtype)
    out = jnp.einsum('blgc,gcd->blgd', fr, w_fmix) + b_fmix
    return out.reshape(b, l, FOURIER_WIDTH)


def attention_branch(q, k, v, q_g, k_g, cos, sin, n_tok):
    b, l, _ = q.shape
    q = q.reshape(b, l, N_Q_HEADS, HEAD_DIM).transpose(0, 2, 1, 3)
    k = k.reshape(b, l, N_KV_HEADS, HEAD_DIM).transpose(0, 2, 1, 3)
    v = v.reshape(b, l, N_KV_HEADS, HEAD_DIM).transpose(0, 2, 1, 3)
    q = apply_rope(rmsnorm(q, q_g), cos, sin)
    k = apply_rope(rmsnorm(k, k_g), cos, sin)
    q = q.reshape(b, N_KV_HEADS, GQA_GROUP, l, HEAD_DIM)
    out_meta = gqa_block(q[:, :, :, :N_META], k, v)
    n_blk = n_tok // Q_BLOCK
    q_real = q[:, :, :, N_META:].reshape(b, N_KV_HEADS, GQA_GROUP, n_blk, Q_BLOCK, HEAD_DIM)
    q_real = q_real.transpose(3, 0, 1, 2, 4, 5)
    out_real = lax.map(lambda qb: gqa_block(qb, k, v), q_real)
    out_real = out_real.transpose(1, 2, 3, 0, 4, 5).reshape(b, N_KV_HEADS, GQA_GROUP, n_tok, HEAD_DIM)
    out = jnp.concatenate([out_meta, out_real], axis=3).reshape(b, N_Q_HEADS, l, HEAD_DIM)
    return out.transpose(0, 2, 1, 3).reshape(b, l, ATTN_WIDTH)


def setup_inputs(seed: int = 0) -> dict:
    key = jax.random.key(seed)
    ks = jax.random.split(key, 10)
    f32 = jnp.float32
    x = jax.random.normal(ks[0], (BATCH, SEQ, D_MODEL), f32)
    meta_tokens = jax.random.normal(ks[1], (N_META, D_MODEL), f32)
    norm_g = 1.0 + 0.05 * jax.random.normal(ks[2], (DEPTH, D_MODEL), f32)
    w_in = jax.random.normal(ks[3], (DEPTH, D_MODEL, IN_WIDTH), f32) * D_MODEL ** -0.5
    w_fmix = jax.random.normal(ks[4], (DEPTH, FOURIER_GROUPS, FOURIER_GROUP_DIM, FOURIER_GROUP_DIM), f32) * FOURIER_GROUP_DIM ** -0.5
    b_fmix = 0.02 * jax.random.normal(ks[5], (DEPTH, FOURIER_GROUPS, FOURIER_GROUP_DIM), f32)
    q_norm_g = 1.0 + 0.05 * jax.random.normal(ks[6], (DEPTH, HEAD_DIM), f32)
    k_norm_g = 1.0 + 0.05 * jax.random.normal(ks[7], (DEPTH, HEAD_DIM), f32)
    w_out = jax.random.normal(ks[8], (DEPTH, MIX_WIDTH, D_MODEL), f32) * MIX_WIDTH ** -0.5
    final_g = 1.0 + 0.05 * jax.random.normal(ks[9], (D_MODEL,), f32)
    return {"x": x, "meta_tokens": meta_tokens, "norm_g": norm_g, "w_in": w_in,
            "w_fmix": w_fmix, "b_fmix": b_fmix, "q_norm_g": q_norm_g, "k_norm_g": k_norm_g,
            "w_out": w_out, "final_g": final_g}


def reference(x, meta_tokens, norm_g, w_in, w_fmix, b_fmix, q_norm_g, k_norm_g, w_out, final_g):
    b, n_tok, d = x.shape
    cos, sin = axial_rope_tables(n_tok)
    meta = jnp.broadcast_to(meta_tokens.astype(x.dtype)[None], (b, N_META, d))
    h = jnp.concatenate([meta, x], axis=1)
    for i in range(DEPTH):
        u = rmsnorm(h, norm_g[i])
        proj = jnp.einsum('bld,de->ble', u, w_in[i])
        f_in, f_gate, q, k, v, a_gate = jnp.split(proj, SPLIT_POINTS, axis=-1)
        f_out = fourier_branch(f_in, w_fmix[i], b_fmix[i]) * jax.nn.silu(f_gate)
        a_out = attention_branch(q, k, v, q_norm_g[i], k_norm_g[i], cos, sin, n_tok) * jax.nn.silu(a_gate)
        mixed = jnp.concatenate([f_out, a_out], axis=-1)
        h = h + jnp.einsum('ble,ed->bld', mixed, w_out[i])
    h = rmsnorm(h, final_g)
    return h[:, N_META:]
```

```python
import contextlib
import math

import numpy as np
import ml_dtypes

import concourse.bass as bass
import concourse.mybir as mybir
from concourse.bass_utils import run_bass_kernel_spmd

F32 = mybir.dt.float32
BF16 = mybir.dt.bfloat16
AF = mybir.ActivationFunctionType
ALU = mybir.AluOpType

D = 1024
SEQ = 8192
NMETA = 16
L = SEQ + NMETA
NT = 65
KC = 8
DEPTH = 4
N1, N2 = 72, 114
G1 = 6
G3 = 4
EPS = 1e-6
C_FA, C_FB, C_FG, C_AG, C_Q, C_QP, C_K, C_KP, C_V, WCOLS = 0, 512, 1024, 1536, 2048, 2560, 3072, 3200, 3328, 3456

ENGS = ("pe", "act", "dve", "pool", "sp")


class Sem:
    def __init__(self, nc, stack, name):
        self.h = stack.enter_context(nc.semaphore(name))
        self.n = 0
        self.name = name


class Buf:
    __slots__ = ("name", "w", "r", "sem")

    def __init__(self, name, sem=None):
        self.name = name
        self.w = None
        self.r = []
        self.sem = sem


class Prog:
    def __init__(self, nc, stack):
        self.nc = nc
        self.stack = stack
        self.streams = {e: [] for e in ENGS}
        self.sems = []
        self.sem_by_name = {}
        self.esem = {}
        for e in ENGS:
            self.esem[e] = self.new_sem("e_" + e)

    def new_sem(self, name):
        if name in self.sem_by_name:
            return self.sem_by_name[name]
        s = Sem(self.nc, self.stack, name)
        self.sems.append(s)
        self.sem_by_name[name] = s
        return s

    def buf(self, name, dma=False):
        return Buf(name, self.new_sem("b_" + name) if dma else None)

    def _deps(self, eng, reads, writes):
        toks = []
        for b in reads:
            if b.w is not None:
                toks.append(b.w)
        for b in writes:
            if b.w is not None and b.w[2] != eng:
                toks.append(b.w)
            for t in b.r:
                if t[2] != eng:
                    toks.append(t)
        return toks

    def op(self, eng, fn, reads=(), writes=()):
        toks = self._deps(eng, reads, writes)
        sem = self.esem[eng]
        sem.n += 1
        tok = (sem, sem.n, eng)
        self.streams[eng].append((toks, fn, sem, 1))
        for b in reads:
            b.r.append(tok)
        for b in writes:
            b.w = tok
            b.r = []
        return tok

    def dma(self, eng, fn, reads=(), writes=(), ndma=1):
        wb = writes[0]
        assert wb.sem is not None, wb.name
        toks = [t for t in self._deps(None, reads, writes) if t[0] is not wb.sem]
        wb.sem.n += 16 * ndma
        tok = (wb.sem, wb.sem.n, "dma")
        self.streams[eng].append((toks, fn, wb.sem, 16))
        for b in reads:
            b.r.append(tok)
        for b in writes:
            b.w = tok
            b.r = []
        return tok

    def barrier(self):
        toks = [(s, s.n, "x") for s in self.sems if s.n > 0]
        for e in ENGS:
            self.streams[e].append((list(toks), None, None, 0))

    def emit(self):
        nc = self.nc
        with nc.Block() as block:
            def run(eng_name):
                def body(e):
                    seen = {}
                    for toks, fn, sem, amt in self.streams[eng_name]:
                        need = {}
                        for (s, v, _) in toks:
                            if seen.get(id(s), 0) >= v:
                                continue
                            if need.get(id(s), (None, 0))[1] < v:
                                need[id(s)] = (s, v)
                        for s, v in need.values():
                            e.wait_ge(s.h, v)
                            seen[id(s)] = v
                        if fn is None:
                            continue
                        r = fn(e)
                        if isinstance(r, (list, tuple)):
                            for ins in r:
                                ins.then_inc(sem.h, amt)
                        else:
                            r.then_inc(sem.h, amt)
                return body
            block.tensor(run("pe"))
            block.scalar(run("act"))
            block.vector(run("dve"))
            block.gpsimd(run("pool"))
            block.sync(run("sp"))


_CONST_CACHE = {}


def _constants():
    if _CONST_CACHE:
        return _CONST_CACHE
    f32 = np.float32
    rows = SEQ // 64
    j = np.arange(SEQ, dtype=np.int32)
    row = (j // 64 - rows // 2).astype(f32)
    col = (j % 64 - 32).astype(f32)
    freqs = (1.0 / (np.float32(10000.0) ** (np.arange(16, dtype=f32) * f32(2.0) / f32(32)))).astype(f32)
    ang = np.concatenate([row[:, None] * freqs, col[:, None] * freqs], axis=-1).astype(f32)
    ang = np.concatenate([np.zeros((NMETA, 32), f32), ang], axis=0)
    cos = np.cos(ang).astype(f32)
    sin = np.sin(ang).astype(f32)
    pair = (np.arange(128) % 64) // 2
    cosT = np.ascontiguousarray(cos[:, pair].T)
    sinT = np.ascontiguousarray(sin[:, pair].T)
    l1 = np.arange(N1, dtype=np.float64)[:, None, None]
    l2 = np.arange(N2, dtype=np.float64)[None, :, None]
    k1 = np.arange(N1, dtype=np.float64)[None, None, :]
    th = 2.0 * np.pi * (l1 * k1 / N1 + l2 * k1 / L)
    m1 = np.concatenate([np.cos(th), np.sin(th), -np.sin(th)], axis=-1)
    m1 = m1.astype(ml_dtypes.bfloat16)
    a = np.arange(N2, dtype=np.float64)
    th3 = 2.0 * np.pi * np.outer(a, a) / N2
    sc = 1.0 / math.sqrt(L * 64.0)
    c3 = np.stack([np.cos(th3) * sc, -np.sin(th3) * sc], axis=1).astype(ml_dtypes.bfloat16)
    c = np.arange(64, dtype=np.float64)
    th64 = 2.0 * np.pi * np.outer(c, c) / 64.0
    cs = np.zeros((128, 2, 128), f32)
    for gl in range(2):
        cs[gl * 64:(gl + 1) * 64, 0, gl * 64:(gl + 1) * 64] = np.cos(th64)
        cs[gl * 64:(gl + 1) * 64, 1, gl * 64:(gl + 1) * 64] = np.sin(th64)
    ident = np.eye(128, dtype=f32)
    bones = np.zeros((128, 128), f32)
    bones[:64, :64] = 1.0
    bones[64:, 64:] = 1.0
    _CONST_CACHE.update(cosT=cosT, sinT=sinT, m1=m1, c3=c3, cs64=cs, ident=ident, bones=bones)
    return _CONST_CACHE


def build_program(n_layers=DEPTH, debug=False, final_norm=True):
    nc = bass.Bass("TRN2", target_bir_lowering=False)
    stack = contextlib.ExitStack()

    def din(name, shape, dt=F32):
        return nc.dram_tensor(name, list(shape), dt, kind="ExternalInput").ap()

    def dscr(name, shape, dt):
        return nc.dram_tensor(name, list(shape), dt, kind="ExternalOutput" if debug else "Internal").ap()

    x_d = din("x", [SEQ, D])
    meta_d = din("meta", [NMETA, D])
    win_d = din("w_in", [DEPTH, KC, 128, 2304])
    wout_d = din("w_out", [DEPTH, 1024, 1024])
    wfm_d = din("w_fmix", [DEPTH, 4, 128, 64])
    ng_d = din("ng", [128, DEPTH * KC])
    qkg_d = din("qkg", [128, DEPTH * 4])
    bfm_d = din("bfm", [128, DEPTH * 4])
    fg_d = din("fg", [128, D])
    cos_d = din("cosT", [128, L])
    sin_d = din("sinT", [128, L])
    m1_d = din("m1", [N1, N2, 216], BF16)
    c3_d = din("c3", [N2, 2, N2], BF16)
    cs64_d = din("cs64", [128, 2, 128])
    ident_d = din("ident", [128, 128])
    bones_d = din("bones", [128, 128])
    out_d = nc.dram_tensor("out", [SEQ, D], F32, kind="ExternalOutput").ap()

    h_d = [dscr("hA", [L, D], F32), dscr("hB", [L, D], F32)]
    z_d = dscr("zs", [L, 1024], BF16)
    b_d = dscr("bs", [N1, N2, 2, 512], BF16)
    gf_d = dscr("gfs", [4, 128, L], BF16)
    ga_d = dscr("gas", [4, 128, L], BF16)
    qt_d = dscr("qts", [4, 128, L], BF16)
    mas_d = dscr("mas", [4, 128, L], BF16) if debug else None

    ARENA_KB = 200
    arena = stack.enter_context(nc.sbuf_tensor("arena", [128, ARENA_KB * 256], F32))
    psum = stack.enter_context(nc.psum_tensor("psum", [128, 7, 512], F32))
    psb = stack.enter_context(nc.psum_tensor("psb", [128, 1024], BF16))

    P = Prog(nc, stack)

    class Carver:
        def __init__(self, base):
            self.off = base

        def take(self, shape, dt):
            esz = 4 if dt == F32 else 2
            n = 1
            for s in shape[1:]:
                n *= s
            nbytes = (n * esz + 3) // 4 * 4
            o4 = self.off // 4
            v = arena[:, o4:o4 + nbytes // 4]
            if dt != F32:
                v = v.bitcast(dt)
            v = v[0:shape[0], 0:n]
            if len(shape) == 3:
                v = v.rearrange("p (a b) -> p a b", b=shape[2])
            elif len(shape) == 4:
                v = v.rearrange("p (a b c) -> p a b c", b=shape[2], c=shape[3])
            self.off += nbytes
            assert self.off <= ARENA_KB * 1024, self.off
            return v

    cv = Carver(0)
    KT = cv.take([128, L], BF16)
    VX = cv.take([128, NT, 192], BF16)
    IDB = cv.take([128, 128], BF16)
    IDF = cv.take([128, 128], F32)
    BONES = cv.take([128, 128], F32)
    CS64 = cv.take([128, 2, 128], F32)
    NG = cv.take([128, DEPTH * KC], F32)
    QKG = cv.take([128, DEPTH * 4], F32)
    BFM = cv.take([128, DEPTH * 4], F32)
    FG = cv.take([128, D], F32)
    EPSQ = cv.take([128, 1], F32)
    EPSN = cv.take([128, 1], F32)
    STAT = cv.take([128, 8], F32)
    PHASE_BASE = cv.off

    b_KT = P.buf("KT")
    b_VX = P.buf("VX")
    b_const = P.buf("const", dma=True)
    b_h = [P.buf("hA", dma=True), P.buf("hB", dma=True)]
    b_z = P.buf("zs", dma=True)
    b_bs = P.buf("bs", dma=True)
    b_gf = P.buf("gfs", dma=True)
    b_ga = P.buf("gas", dma=True)
    b_qt = P.buf("qts", dma=True)
    b_out = P.buf("out", dma=True)
    b_mas = P.buf("mas", dma=True)
    b_ps = [P.buf("ps%d" % i) for i in range(7)]
    b_psb = P.buf("psb")
    b_idb = P.buf("idb")
    b_stat = P.buf("stat")

    def ld_consts(e):
        return [
            e.dma_start(out=IDF, in_=ident_d),
            e.dma_start(out=BONES, in_=bones_d),
            e.dma_start(out=CS64, in_=cs64_d),
            e.dma_start(out=NG, in_=ng_d),
            e.dma_start(out=QKG, in_=qkg_d),
            e.dma_start(out=BFM, in_=bfm_d),
            e.dma_start(out=FG, in_=fg_d),
        ]
    P.dma("sp", ld_consts, writes=[b_const], ndma=7)

    def init_h(e):
        return [
            e.dma_start(out=h_d[0][0:NMETA, :], in_=meta_d),
            e.dma_start(out=h_d[0][NMETA:NMETA + 4096, :], in_=x_d[0:4096, :]),
            e.dma_start(out=h_d[0][NMETA + 4096:L, :], in_=x_d[4096:SEQ, :]),
        ]
    P.dma("pool", init_h, writes=[b_h[0]], ndma=3)

    def init_dve(e):
        e.memset(EPSQ, EPS)
        e.memset(EPSN, EPS)
        e.memset(VX[:, :, 64:128], 1.0)
        return e.tensor_copy(IDB, IDF)
    P.op("dve", init_dve, reads=[b_const], writes=[b_idb, b_VX, b_stat])

    def tiles_of_block(b):
        if b < 16:
            return [(4 * b + i, 512 * b + 128 * i, 128) for i in range(4)]
        return [(64, 8192, 16)]

    NBLK = 17

    def layer_body(layer):
        hin, hout = h_d[layer % 2], h_d[(layer + 1) % 2]
        b_hin, b_hout = b_h[layer % 2], b_h[(layer + 1) % 2]
        last = (layer == n_layers - 1)

        P.barrier()
        cv = Carver(PHASE_BASE)
        WB = cv.take([128, KC, WCOLS], BF16)
        P1_BASE = cv.off
        WS = [cv.take([128, 2304], F32) for _ in range(2)]
        WT = [cv.take([128, 128], F32) for _ in range(2)]
        GB = cv.take([128, 4, 2, 128], F32)
        WFM = cv.take([128, 4, 64], F32)
        b_WB = P.buf("WB")
        b_WS = [P.buf("WS0", dma=True), P.buf("WS1", dma=True)]
        b_WT = [P.buf("WT0"), P.buf("WT1")]
        b_GB = P.buf("GB")
        b_WFM = P.buf("WFM", dma=True)

        P.dma("sp", lambda e: e.dma_start(out=WFM, in_=wfm_d[layer].rearrange("j p d -> p j d")),
              writes=[b_WFM])
        P.op("dve", lambda e: e.memset(GB, 0.0), writes=[b_GB])
        for j in range(4):
            for cs in range(2):
                pb = (2 * j + cs) % 4
                P.op("pe", lambda e, j=j, cs=cs, pb=pb: e.matmul(
                    psum[:, pb, 0:64], CS64[:, cs, :], WFM[:, j, :], start=True, stop=True),
                    reads=[b_const, b_WFM], writes=[b_ps[pb]])

                def evac_g(e, j=j, cs=cs, pb=pb):
                    e.tensor_copy(GB[0:64, j, cs, 0:64], psum[0:64, pb, 0:64])
                    return e.tensor_copy(GB[64:128, j, cs, 64:128], psum[64:128, pb, 0:64])
                P.op("dve", evac_g, reads=[b_ps[pb]], writes=[b_GB])

        qg_i = 4 * layer
        for kc in range(KC):
            s = kc % 2
            ws = WS[s]
            P.dma("sp", lambda e, kc=kc, ws=ws: e.dma_start(out=ws, in_=win_d[layer, kc]), writes=[b_WS[s]])
            gn = NG[:, layer * KC + kc: layer * KC + kc + 1]

            def cast_cols(e, kc=kc, ws=ws, gn=gn):
                wb = WB[:, kc, :]
                e.tensor_scalar_mul(wb[:, C_FG:C_FG + 512], ws[:, 512:1024], gn)
                for (dst, src) in ((C_AG, 1792), (C_Q, 1024)):
                    e.tensor_scalar_mul(
                        wb[:, dst:dst + 512].rearrange("p (c h d) -> p c h d", c=4, h=2, d=64),
                        ws[:, src:src + 512].rearrange("p (h c d) -> p c h d", h=2, c=4, d=64), gn)
                for half in range(2):
                    srcv = ws[:, 1024 + half * 256: 1024 + (half + 1) * 256].rearrange(
                        "p (c i t) -> p c i t", c=4, i=32, t=2)
                    dstv = wb[:, C_QP:C_QP + 512].rearrange("p (c h i t) -> p c h i t", c=4, h=2, i=32, t=2)[:, :, half]
                    e.tensor_scalar(dstv[:, :, :, 0], srcv[:, :, :, 1], gn, -1.0, ALU.mult, ALU.mult)
                    e.tensor_scalar_mul(dstv[:, :, :, 1], srcv[:, :, :, 0], gn)
                e.tensor_scalar_mul(wb[:, C_K:C_K + 128], ws[:, 1536:1664], gn)
                ksv = ws[:, 1536:1664].rearrange("p (i t) -> p i t", t=2)
                kdv = wb[:, C_KP:C_KP + 128].rearrange("p (i t) -> p i t", t=2)
                e.tensor_scalar(kdv[:, :, 0], ksv[:, :, 1], gn, -1.0, ALU.mult, ALU.mult)
                e.tensor_scalar_mul(kdv[:, :, 1], ksv[:, :, 0], gn)
                return e.tensor_scalar_mul(wb[:, C_V:C_V + 128], ws[:, 1664:1792], gn)
            P.op("dve", cast_cols, reads=[b_WS[s], b_const], writes=[b_WB])

            for j in range(4):
                t = (kc * 4 + j) % 2
                pb = (kc * 4 + j) % 2
                pm = 2 + (kc * 4 + j) % 2
                P.op("pe", lambda e, ws=ws, j=j, pb=pb: e.transpose(
                    psum[:, pb, 0:128], ws[:, j * 128:(j + 1) * 128], IDF),
                    reads=[b_WS[s], b_const], writes=[b_ps[pb]])
                P.op("dve", lambda e, t=t, pb=pb: e.tensor_copy(WT[t], psum[:, pb, 0:128]),
                     reads=[b_ps[pb]], writes=[b_WT[t]])
                P.op("pe", lambda e, t=t, j=j, pm=pm: e.matmul(
                    psum[:, pm, 0:256], WT[t], GB[:, j].rearrange("p a b -> p (a b)"), start=True, stop=True),
                    reads=[b_WT[t], b_GB], writes=[b_ps[pm]])

                def evac_w(e, kc=kc, j=j, pm=pm, gn=gn):
                    e.tensor_scalar_mul(WB[:, kc, C_FA + j * 128: C_FA + (j + 1) * 128], psum[:, pm, 0:128], gn)
                    return e.tensor_scalar_mul(WB[:, kc, C_FB + j * 128: C_FB + (j + 1) * 128], psum[:, pm, 128:256], gn)
                P.op("dve", evac_w, reads=[b_ps[pm], b_const], writes=[b_WB])

        P.barrier()
        cv = Carver(P1_BASE)
        HT = [cv.take([128, D], F32) for _ in range(2)]
        SQJ = cv.take([128, D], BF16)
        U = [cv.take([128, D], BF16) for _ in range(2)]
        UT = [cv.take([128, KC, 512], BF16) for _ in range(2)]
        ZS = [cv.take([128, 1024], BF16) for _ in range(2)]
        GS = [cv.take([128, 4, 512], BF16) for _ in range(2)]
        QS = [cv.take([128, 4, 512], BF16) for _ in range(2)]
        COS = [cv.take([128, 512], F32) for _ in range(2)]
        SIN = [cv.take([128, 512], F32) for _ in range(2)]
        SQ = [cv.take([128, 512], F32) for _ in range(2)]
        RSTD = [cv.take([128, 512], F32) for _ in range(2)]
        T1 = [cv.take([128, 512], F32) for _ in range(2)]
        T2 = [cv.take([128, 512], F32) for _ in range(2)]
        b_HT = [P.buf("HT%d" % i, dma=True) for i in range(2)]
        b_SQJ = P.buf("SQJ")
        b_U = [P.buf("U%d" % i) for i in range(2)]
        b_UT = [P.buf("UT%d" % i) for i in range(2)]
        b_ZS = [P.buf("ZS%d" % i) for i in range(2)]
        b_GS = [P.buf("GS%d" % i) for i in range(2)]
        b_QS = [P.buf("QS%d" % i) for i in range(2)]
        b_CS = [P.buf("CS%d" % i, dma=True) for i in range(2)]
        b_SQ = [P.buf("SQ%d" % i) for i in range(2)]
        b_RSTD = [P.buf("RSTD%d" % i) for i in range(2)]
        b_T1 = [P.buf("T1%d" % i) for i in range(2)]
        b_T2 = [P.buf("T2%d" % i) for i in range(2)]
        b_ss = [P.buf("ss%d" % i) for i in range(2)]

        tcount_l = [0]

        def front_tables(bb):
            tl = tiles_of_block(bb)
            nb = sum(r for (_, _, r) in tl)
            tok0 = tl[0][1]
            cs_s = bb % 2

            def ld_cs(e):
                return [e.dma_start(out=COS[cs_s][:, 0:nb], in_=cos_d[:, tok0:tok0 + nb]),
                        e.dma_start(out=SIN[cs_s][:, 0:nb], in_=sin_d[:, tok0:tok0 + nb])]
            P.dma("sp", ld_cs, writes=[b_CS[cs_s]], ndma=2)

        def front_tile(bb, idx):
            tl = tiles_of_block(bb)
            tok0 = tl[0][1]
            us = bb % 2
            (ti, t0, tr) = tl[idx]
            hs = tcount_l[0] % 2
            tcount_l[0] += 1
            loc = t0 - tok0
            P.dma("sp", lambda e: e.dma_start(out=HT[hs][0:tr, :], in_=hin[t0:t0 + tr, :]),
                  reads=[b_hin], writes=[b_HT[hs]])
            P.op("act", lambda e: e.activation(
                out=SQJ[0:tr, :], in_=HT[hs][0:tr, :], func=AF.Square, accum_out=STAT[0:tr, hs:hs + 1]),
                reads=[b_HT[hs]], writes=[b_SQJ, b_ss[hs]])
            P.op("act", lambda e: e.activation(
                out=STAT[0:tr, 2 + hs:3 + hs], in_=STAT[0:tr, hs:hs + 1], func=AF.Ln, bias=EPSN[0:tr, :], scale=1.0 / D),
                reads=[b_ss[hs], b_stat], writes=[b_ss[hs]])
            P.op("act", lambda e: e.activation(
                out=STAT[0:tr, 4 + hs:5 + hs], in_=STAT[0:tr, 2 + hs:3 + hs], func=AF.Exp, scale=-0.5),
                reads=[b_ss[hs]], writes=[b_ss[hs]])
            P.op("dve", lambda e: e.tensor_scalar_mul(
                U[hs][0:tr, :], HT[hs][0:tr, :], STAT[0:tr, 4 + hs:5 + hs]),
                reads=[b_HT[hs], b_ss[hs]], writes=[b_U[hs]])

            def tr_u(e):
                r = None
                for kc in range(KC):
                    r = e.transpose(psb[:, kc * 128: kc * 128 + tr], U[hs][0:tr, kc * 128:(kc + 1) * 128], IDB[0:tr, 0:tr])
                return r
            P.op("pe", tr_u, reads=[b_U[hs], b_idb], writes=[b_psb])
            P.op("dve", lambda e: e.tensor_copy(
                UT[us][:, :, loc:loc + tr], psb[:, :].rearrange("p (k t) -> p k t", t=128)[:, :, 0:tr]),
                reads=[b_psb], writes=[b_UT[us]])

        front_tables(0)
        for _idx in range(len(tiles_of_block(0))):
            front_tile(0, _idx)
        tcount = 0
        mmc = 0
        gcount = 0
        qcount = 0
        for b in range(NBLK):
            tl = tiles_of_block(b)
            nb = sum(r for (_, _, r) in tl)
            tok0 = tl[0][1]
            us = b % 2
            cs_s = b % 2
            for (ti, t0, tr) in tl:
                loc = t0 - tok0
                zs = ti % 2
                for half in range(2):
                    pb = mmc % 3
                    mmc += 1

                    def mm_z(e, us=us, loc=loc, tr=tr, half=half, pb=pb):
                        r = None
                        for kc in range(KC):
                            r = e.matmul(psum[0:tr, pb, :], UT[us][:, kc, loc:loc + tr],
                                         WB[:, kc, half * 512:(half + 1) * 512], start=(kc == 0), stop=(kc == KC - 1))
                        return r
                    P.op("pe", mm_z, reads=[b_UT[us], b_WB], writes=[b_ps[pb]])
                    P.op("dve", lambda e, zs=zs, tr=tr, half=half, pb=pb: e.tensor_copy(
                        ZS[zs][0:tr, half * 512:(half + 1) * 512], psum[0:tr, pb, :]),
                        reads=[b_ps[pb]], writes=[b_ZS[zs]])
                P.dma("pool", lambda e, zs=zs, t0=t0, tr=tr: e.dma_start(out=z_d[t0:t0 + tr, :], in_=ZS[zs][0:tr, :]),
                      reads=[b_ZS[zs]], writes=[b_z])
                pb = mmc % 3
                mmc += 1

                def mm_v(e, us=us, loc=loc, tr=tr, pb=pb):
                    r = None
                    for kc in range(KC):
                        r = e.matmul(psum[0:tr, pb, 0:128], UT[us][:, kc, loc:loc + tr],
                                     WB[:, kc, C_V:C_V + 128], start=(kc == 0), stop=(kc == KC - 1))
                    return r
                P.op("pe", mm_v, reads=[b_UT[us], b_WB], writes=[b_ps[pb]])

                def ev_v(e, ti=ti, tr=tr, pb=pb):
                    e.tensor_copy(VX[0:tr, ti, 0:64], psum[0:tr, pb, 0:64])
                    return e.tensor_copy(VX[0:tr, ti, 128:192], psum[0:tr, pb, 64:128])
                P.op("dve", ev_v, reads=[b_ps[pb]], writes=[b_VX])
                _k = ti - tl[0][0]
                if b + 1 < NBLK:
                    if _k == 0:
                        front_tables(b + 1)
                    if _k < len(tiles_of_block(b + 1)):
                        front_tile(b + 1, _k)

            for gi, (col0, dst_d, b_dst) in enumerate(((C_FG, gf_d, b_gf), (C_AG, ga_d, b_ga))):
                gs = gcount % 2
                gcount += 1
                for j in range(4):
                    pb = mmc % 3
                    mmc += 1

                    def mm_g(e, us=us, nb=nb, col=col0 + j * 128, pb=pb):
                        r = None
                        for kc in range(KC):
                            r = e.matmul(psum[:, pb, 0:nb], WB[:, kc, col:col + 128], UT[us][:, kc, 0:nb],
                                         start=(kc == 0), stop=(kc == KC - 1))
                        return r
                    P.op("pe", mm_g, reads=[b_UT[us], b_WB], writes=[b_ps[pb]])
                    P.op("act", lambda e, gs=gs, j=j, nb=nb, pb=pb: e.activation(
                        out=GS[gs][:, j, 0:nb], in_=psum[:, pb, 0:nb], func=AF.Silu),
                        reads=[b_ps[pb]], writes=[b_GS[gs]])
                P.dma("pool", lambda e, gs=gs, dst_d=dst_d, tok0=tok0, nb=nb: e.dma_start(
                    out=dst_d[:, :, tok0:tok0 + nb].rearrange("c p t -> p c t"), in_=GS[gs][:, :, 0:nb]),
                    reads=[b_GS[gs]], writes=[b_dst])

            qs = b % 2
            for ci in range(5):
                isk = (ci == 4)
                col = C_K if isk else C_Q + ci * 128
                colp = C_KP if isk else C_QP + ci * 128
                gcol = qg_i + (2 if isk else 0)
                w = qcount % 2
                qa = 3 + 2 * (qcount % 2)
                qb = qa + 1
                qcount += 1
                pss = mmc % 3
                mmc += 1

                def mm_q(e, us=us, nb=nb, col=col, colp=colp, qa=qa, qb=qb):
                    r = None
                    for kc in range(KC):
                        e.matmul(psum[:, qa, 0:nb], WB[:, kc, col:col + 128], UT[us][:, kc, 0:nb],
                                 start=(kc == 0), stop=(kc == KC - 1))
                    for kc in range(KC):
                        r = e.matmul(psum[:, qb, 0:nb], WB[:, kc, colp:colp + 128], UT[us][:, kc, 0:nb],
                                     start=(kc == 0), stop=(kc == KC - 1))
                    return r
                P.op("pe", mm_q, reads=[b_UT[us], b_WB], writes=[b_ps[qa], b_ps[qb]])
                P.op("act", lambda e, w=w, nb=nb, qa=qa: e.activation(out=SQ[w][:, 0:nb], in_=psum[:, qa, 0:nb], func=AF.Square),
                     reads=[b_ps[qa]], writes=[b_SQ[w]])
                P.op("pe", lambda e, w=w, nb=nb, pss=pss: e.matmul(psum[:, pss, 0:nb], BONES, SQ[w][:, 0:nb], start=True, stop=True),
                     reads=[b_SQ[w], b_const], writes=[b_ps[pss]])
                P.op("act", lambda e, w=w, nb=nb, pss=pss: e.activation(
                    out=SQ[w][:, 0:nb], in_=psum[:, pss, 0:nb], func=AF.Ln, bias=EPSQ, scale=1.0 / 64),
                    reads=[b_ps[pss], b_stat], writes=[b_SQ[w]])
                P.op("act", lambda e, w=w, nb=nb: e.activation(
                    out=RSTD[w][:, 0:nb], in_=SQ[w][:, 0:nb], func=AF.Exp, scale=-0.5),
                    reads=[b_SQ[w]], writes=[b_RSTD[w]])
                P.op("dve", lambda e, w=w, nb=nb, gcol=gcol, qa=qa: e.scalar_tensor_tensor(
                    out=T1[w][:, 0:nb], in0=psum[:, qa, 0:nb], scalar=QKG[:, gcol:gcol + 1], in1=RSTD[w][:, 0:nb],
                    op0=ALU.mult, op1=ALU.mult),
                    reads=[b_ps[qa], b_RSTD[w], b_const], writes=[b_T1[w]])
                P.op("dve", lambda e, w=w, nb=nb, gcol=gcol, qb=qb: e.scalar_tensor_tensor(
                    out=T2[w][:, 0:nb], in0=psum[:, qb, 0:nb], scalar=QKG[:, gcol + 1:gcol + 2], in1=RSTD[w][:, 0:nb],
                    op0=ALU.mult, op1=ALU.mult),
                    reads=[b_ps[qb], b_RSTD[w], b_const], writes=[b_T2[w]])
                P.op("dve", lambda e, w=w, nb=nb, cs_s=cs_s: e.tensor_tensor(
                    T1[w][:, 0:nb], T1[w][:, 0:nb], COS[cs_s][:, 0:nb], ALU.mult),
                    reads=[b_T1[w], b_CS[cs_s]], writes=[b_T1[w]])
                P.op("dve", lambda e, w=w, nb=nb, cs_s=cs_s: e.tensor_tensor(
                    T2[w][:, 0:nb], T2[w][:, 0:nb], SIN[cs_s][:, 0:nb], ALU.mult),
                    reads=[b_T2[w], b_CS[cs_s]], writes=[b_T2[w]])
                if isk:
                    P.op("dve", lambda e, w=w, nb=nb, tok0=tok0: e.tensor_tensor(
                        KT[:, tok0:tok0 + nb], T1[w][:, 0:nb], T2[w][:, 0:nb], ALU.add),
                        reads=[b_T1[w], b_T2[w]], writes=[b_KT])
                else:
                    P.op("dve", lambda e, w=w, nb=nb, qs=qs, ci=ci: e.tensor_tensor(
                        QS[qs][:, ci, 0:nb], T1[w][:, 0:nb], T2[w][:, 0:nb], ALU.add),
                        reads=[b_T1[w], b_T2[w]], writes=[b_QS[qs]])
            P.dma("pool", lambda e, qs=qs, tok0=tok0, nb=nb: e.dma_start(
                out=qt_d[:, :, tok0:tok0 + nb].rearrange("c p t -> p c t"), in_=QS[qs][:, :, 0:nb]),
                reads=[b_QS[qs]], writes=[b_qt])

        P.barrier()
        cv = Carver(PHASE_BASE)
        GF = cv.take([128, 4, L], BF16)
        ZT = [cv.take([N1, G1, 1024], BF16) for _ in range(2)]
        M1 = [cv.take([N1, G1, 216], BF16) for _ in range(2)]
        BT = [cv.take([N1, G1, 2, 512], BF16) for _ in range(2)]
        BL = [cv.take([N2, G3, 1024], BF16) for _ in range(2)]
        C3 = cv.take([N2, 2, N2], BF16)
        b_GF = P.buf("GF", dma=True)
        b_ZT = [P.buf("ZT%d" % i, dma=True) for i in range(2)]
        b_BT = [P.buf("BT%d" % i) for i in range(2)]
        b_BL = [P.buf("BL%d" % i, dma=True) for i in range(2)]
        b_C3 = P.buf("C3", dma=True)

        P.dma("sp", lambda e: e.dma_start(out=C3, in_=c3_d), writes=[b_C3])
        P.dma("sp", lambda e: [e.dma_start(out=GF[:, c, :], in_=gf_d[c]) for c in range(4)],
              reads=[b_gf], writes=[b_GF], ndma=4)

        z_v = z_d.rearrange("(a b) c -> a b c", b=N2)
        for gi in range(N2 // G1):
            s = gi % 2

            def ld_z(e, s=s, gi=gi):
                return [e.dma_start(out=ZT[s], in_=z_v[:, gi * G1:(gi + 1) * G1, :]),
                        e.dma_start(out=M1[s], in_=m1_d[:, gi * G1:(gi + 1) * G1, :])]
            P.dma("sp", ld_z, reads=[b_z], writes=[b_ZT[s]], ndma=2)
            for g in range(G1):
                pr = (gi * G1 + g) % 2
                pi = 2 + (gi * G1 + g) % 2

                def mm_s1(e, s=s, g=g, pr=pr, pi=pi):
                    fa = ZT[s][:, g, 0:512]
                    fb = ZT[s][:, g, 512:1024]
                    e.matmul(psum[0:N1, pr, :], M1[s][:, g, 0:72], fa, start=True, stop=False)
                    e.matmul(psum[0:N1, pr, :], M1[s][:, g, 144:216], fb, start=False, stop=True)
                    e.matmul(psum[0:N1, pi, :], M1[s][:, g, 72:144], fa, start=True, stop=False)
                    return e.matmul(psum[0:N1, pi, :], M1[s][:, g, 0:72], fb, start=False, stop=True)
                P.op("pe", mm_s1, reads=[b_ZT[s]], writes=[b_ps[pr], b_ps[pi]])
                P.op("dve", lambda e, s=s, g=g, pr=pr: e.tensor_copy(BT[s][:, g, 0, :], psum[0:N1, pr, :]),
                     reads=[b_ps[pr]], writes=[b_BT[s]])
                P.op("act", lambda e, s=s, g=g, pi=pi: e.activation(out=BT[s][:, g, 1, :], in_=psum[0:N1, pi, :], func=AF.Copy),
                     reads=[b_ps[pi]], writes=[b_BT[s]])
            P.dma("pool", lambda e, s=s, gi=gi: e.dma_start(out=b_d[:, gi * G1:(gi + 1) * G1, :, :], in_=BT[s]),
                  reads=[b_BT[s]], writes=[b_bs])

        b_v = b_d.rearrange("k l r c -> l k (r c)")
        poc = 0
        for kg in range(N1 // G3):
            s = kg % 2
            P.dma("sp", lambda e, s=s, kg=kg: e.dma_start(out=BL[s], in_=b_v[:, kg * G3:(kg + 1) * G3, :]),
                  reads=[b_bs], writes=[b_BL[s]])
            for c in range(4):
                pb = 4 + poc % 3
                poc += 1
                pov = psum[:, pb, 0:G3 * N2].rearrange("p (a k) -> p a k", k=N2)

                def mm_s3(e, s=s, c=c, pov=pov):
                    r = None
                    for a in range(G3):
                        e.matmul(pov[:, a, :], BL[s][:, a, c * 128:(c + 1) * 128], C3[:, 0, :], start=True, stop=False)
                        r = e.matmul(pov[:, a, :], BL[s][:, a, 512 + c * 128: 512 + (c + 1) * 128], C3[:, 1, :],
                                     start=False, stop=True)
                    return r
                P.op("pe", mm_s3, reads=[b_BL[s], b_C3], writes=[b_ps[pb]])
                gview = GF[:, c, :].rearrange("p (k2 k1) -> p k1 k2", k1=N1)[:, kg * G3:(kg + 1) * G3, :]
                P.op("dve", lambda e, c=c, pov=pov, gview=gview: e.scalar_tensor_tensor(
                    out=gview, in0=pov, scalar=BFM[:, layer * 4 + c: layer * 4 + c + 1], in1=gview,
                    op0=ALU.add, op1=ALU.mult),
                    reads=[b_ps[pb], b_GF, b_const], writes=[b_GF])
        P.dma("pool", lambda e: [e.dma_start(out=gf_d[c], in_=GF[:, c, :]) for c in range(4)],
              reads=[b_GF], writes=[b_gf], ndma=4)

        P.barrier()
        cv = Carver(PHASE_BASE)
        WO = cv.take([128, KC, D], BF16)
        WOS = [cv.take([128, D], F32) for _ in range(2)]
        QT = [cv.take([128, 4, 512], BF16) for _ in range(2)]
        GA = [cv.take([128, 4, 512], BF16) for _ in range(2)]
        GFB = [cv.take([128, 4, 512], BF16) for _ in range(3)]
        MA = [cv.take([128, 4, 512], BF16) for _ in range(2)]
        NPT = 3
        PT = [cv.take([128, 3, 512], BF16) for _ in range(NPT)]
        HT3 = [cv.take([128, D], F32) for _ in range(2)]
        HN = [cv.take([128, D], F32) for _ in range(2)]
        RC = [cv.take([128, 512], F32) for _ in range(2)]
        TMP = [cv.take([128, 512], F32) for _ in range(2)]
        SQJ3 = cv.take([128, D], BF16)
        b_WO = P.buf("WO")
        b_WOS = [P.buf("WOS%d" % i, dma=True) for i in range(2)]
        b_QT = [P.buf("QT%d" % i, dma=True) for i in range(2)]
        b_GA = [P.buf("GA%d" % i, dma=True) for i in range(2)]
        b_GFB = [P.buf("GFB%d" % i, dma=True) for i in range(3)]
        b_MA = [P.buf("MA%d" % i) for i in range(2)]
        b_PT = [P.buf("PT%d" % i) for i in range(NPT)]
        b_HT3 = [P.buf("HT3%d" % i, dma=True) for i in range(2)]
        b_HN = [P.buf("HN%d" % i) for i in range(2)]
        b_RC = [P.buf("RC%d" % i) for i in range(2)]
        b_TMP = [P.buf("TMP%d" % i) for i in range(2)]
        b_SQJ3 = P.buf("SQJ3")
        b_st3 = [P.buf("st3%d" % i) for i in range(2)]
        b_ST = [P.buf("STa"), P.buf("STb")]

        for kc in range(KC):
            s = kc % 2
            if kc < 4:
                P.dma("sp", lambda e, s=s, kc=kc: e.dma_start(out=WOS[s], in_=wout_d[layer, kc * 128:(kc + 1) * 128, :]),
                      writes=[b_WOS[s]])
            else:
                c = kc - 4

                def ld_wo(e, s=s, c=c):
                    return [e.dma_start(out=WOS[s][0:64, :], in_=wout_d[layer, 512 + 64 * c: 512 + 64 * (c + 1), :]),
                            e.dma_start(out=WOS[s][64:128, :], in_=wout_d[layer, 512 + 64 * (4 + c): 512 + 64 * (5 + c), :])]
                P.dma("sp", ld_wo, writes=[b_WOS[s]], ndma=2)
            P.op("dve", lambda e, s=s, kc=kc: e.tensor_copy(WO[:, kc, :], WOS[s]), reads=[b_WOS[s]], writes=[b_WO])

        blk = []
        for b in range(NBLK):
            tl = tiles_of_block(b)
            blk.append((tl, sum(r for (_, _, r) in tl), tl[0][1]))
        steps = []
        for b in range(NBLK):
            for c in range(4):
                for t in range(NT):
                    steps.append((b, c, t))
        nsteps = len(steps)
        b_O = [b_ps[4], b_ps[5]]
        OPB = 6

        def emit_block_loads(b):
            tl, nb, tok0 = blk[b]
            s = b % 2
            s3 = b % 3
            P.dma("sp", lambda e: e.dma_start(
                out=QT[s][:, :, 0:nb], in_=qt_d[:, :, tok0:tok0 + nb].rearrange("c p t -> p c t")),
                reads=[b_qt], writes=[b_QT[s]])
            P.dma("sp", lambda e: e.dma_start(
                out=GA[s][:, :, 0:nb], in_=ga_d[:, :, tok0:tok0 + nb].rearrange("c p t -> p c t")),
                reads=[b_ga], writes=[b_GA[s]])
            P.dma("sp", lambda e: e.dma_start(
                out=GFB[s3][:, :, 0:nb], in_=gf_d[:, :, tok0:tok0 + nb].rearrange("c p t -> p c t")),
                reads=[b_gf], writes=[b_GFB[s3]])

        def emit_qk(i):
            b, c, t = steps[i]
            tl, nb, tok0 = blk[b]
            s = b % 2
            sg = i % 2
            bank0 = 2 * sg
            kr = 128 if t < 64 else 16

            def mm_qk(e):
                e.matmul(psum[0:kr, bank0, 0:nb], KT[0:64, t * 128: t * 128 + kr], QT[s][0:64, c, 0:nb],
                         start=True, stop=True)
                return e.matmul(psum[0:kr, bank0 + 1, 0:nb], KT[64:128, t * 128: t * 128 + kr], QT[s][64:128, c, 0:nb],
                                start=True, stop=True)
            P.op("pe", mm_qk, reads=[b_KT, b_QT[s]], writes=[b_ST[sg]])

        def emit_exp_pv(i):
            b, c, t = steps[i]
            tl, nb, tok0 = blk[b]
            sg = i % 2
            pt = i % NPT
            bank0 = 2 * sg
            kr = 128 if t < 64 else 16
            P.op("act", lambda e: e.activation(out=PT[pt][0:kr, 0:2, 0:nb], in_=psum[0:kr, bank0:bank0 + 2, 0:nb],
                                               func=AF.Exp, scale=0.125),
                 reads=[b_ST[sg]], writes=[b_PT[pt]])

            def mm_pv(e):
                e.matmul(psum[:, 4, 0:nb], VX[0:kr, t, 0:128], PT[pt][0:kr, 0, 0:nb], start=(t == 0), stop=(t == NT - 1))
                return e.matmul(psum[:, 5, 0:nb], VX[0:kr, t, 64:192], PT[pt][0:kr, 1, 0:nb],
                                start=(t == 0), stop=(t == NT - 1))
            P.op("pe", mm_pv, reads=[b_VX, b_PT[pt]], writes=[b_O[0], b_O[1]])

        def emit_chunk_norm(b, c):
            tl, nb, tok0 = blk[b]
            s = b % 2
            for half in range(2):
                r0 = half * 64
                nr = slice(r0, r0 + 64)
                dr = slice(64 - r0, 128 - r0)
                ob = 4 + half
                P.op("act", lambda e, dr=dr, ob=ob, half=half: e.activation(
                    out=RC[half][dr, 0:nb], in_=psum[dr, ob, 0:nb], func=AF.Ln),
                    reads=[b_O[half]], writes=[b_RC[half]])
                P.op("act", lambda e, dr=dr, half=half: e.activation(
                    out=RC[half][dr, 0:nb], in_=RC[half][dr, 0:nb], func=AF.Exp, scale=-1.0),
                    reads=[b_RC[half]], writes=[b_RC[half]])
                P.op("dve", lambda e, dr=dr, nr=nr, ob=ob, half=half: e.tensor_tensor(
                    TMP[half][nr, 0:nb], psum[nr, ob, 0:nb], RC[half][dr, 0:nb], ALU.mult),
                    reads=[b_O[half], b_RC[half]], writes=[b_TMP[half]])
                P.op("dve", lambda e, nr=nr, half=half: e.tensor_tensor(
                    MA[s][nr, c, 0:nb], TMP[half][nr, 0:nb], GA[s][nr, c, 0:nb], ALU.mult),
                    reads=[b_TMP[half], b_GA[s]], writes=[b_MA[s]])
                if debug:
                    P.dma("pool", lambda e, nr=nr: e.dma_start(out=mas_d[c, nr, tok0:tok0 + nb], in_=MA[s][nr, c, 0:nb]),
                          reads=[b_MA[s]], writes=[b_mas])

        t3count = [0]

        def make_oproj_groups(b):
            tl, nb, tok0 = blk[b]
            s = b % 2
            s3 = b % 3
            out = []
            for (ti, t0, tr) in tl:
                loc = t0 - tok0
                hs = t3count[0] % 2
                t3count[0] += 1
                for half in range(2):
                    def grp_fn(ti=ti, t0=t0, tr=tr, loc=loc, hs=hs, half=half):
                        if half == 0:
                            P.dma("sp", lambda e: e.dma_start(out=HT3[hs][0:tr, :], in_=hin[t0:t0 + tr, :]),
                                  reads=[b_hin], writes=[b_HT3[hs]])
                        PSO = psum[:, OPB, :]

                        def mm_o(e):
                            r = None
                            for kc in range(KC):
                                lhsT = GFB[s3][:, kc, loc:loc + tr] if kc < 4 else MA[s][:, kc - 4, loc:loc + tr]
                                r = e.matmul(PSO[0:tr, :], lhsT, WO[:, kc, half * 512:(half + 1) * 512],
                                             start=(kc == 0), stop=(kc == KC - 1))
                            return r
                        P.op("pe", mm_o, reads=[b_GFB[s3], b_MA[s], b_WO], writes=[b_ps[OPB]])
                        P.op("dve", lambda e: e.tensor_tensor(
                            HN[hs][0:tr, half * 512:(half + 1) * 512], PSO[0:tr, :], HT3[hs][0:tr, half * 512:(half + 1) * 512],
                            ALU.add),
                            reads=[b_ps[OPB], b_HT3[hs]], writes=[b_HN[hs]])
                        if half == 0:
                            return
                        lo = max(t0, NMETA)
                        if not last:
                            P.dma("pool", lambda e: e.dma_start(out=hout[t0:t0 + tr, :], in_=HN[hs][0:tr, :]),
                                  reads=[b_HN[hs]], writes=[b_hout])
                            return
                        if final_norm:
                            P.op("act", lambda e: e.activation(
                                out=SQJ3[0:tr, :], in_=HN[hs][0:tr, :], func=AF.Square, accum_out=STAT[0:tr, hs:hs + 1]),
                                reads=[b_HN[hs]], writes=[b_SQJ3, b_st3[hs]])
                            P.op("act", lambda e: e.activation(
                                out=STAT[0:tr, 2 + hs:3 + hs], in_=STAT[0:tr, hs:hs + 1], func=AF.Ln, bias=EPSN[0:tr, :],
                                scale=1.0 / D),
                                reads=[b_st3[hs], b_stat], writes=[b_st3[hs]])
                            P.op("act", lambda e: e.activation(
                                out=STAT[0:tr, 4 + hs:5 + hs], in_=STAT[0:tr, 2 + hs:3 + hs], func=AF.Exp, scale=-0.5),
                                reads=[b_st3[hs]], writes=[b_st3[hs]])
                            P.op("dve", lambda e: e.scalar_tensor_tensor(
                                out=HN[hs][0:tr, :], in0=HN[hs][0:tr, :], scalar=STAT[0:tr, 4 + hs:5 + hs], in1=FG[0:tr, :],
                                op0=ALU.mult, op1=ALU.mult),
                                reads=[b_HN[hs], b_st3[hs], b_const], writes=[b_HN[hs]])
                        P.dma("pool", lambda e: e.dma_start(
                            out=out_d[lo - NMETA:t0 + tr - NMETA, :], in_=HN[hs][lo - t0:tr, :]),
                            reads=[b_HN[hs]], writes=[b_out])
                    out.append(grp_fn)
            return out

        deferred = []
        emit_block_loads(0)
        emit_qk(0)
        for i in range(nsteps):
            b, c, t = steps[i]
            if c == 0 and t == 0 and b + 1 < NBLK:
                emit_block_loads(b + 1)
            if i + 1 < nsteps:
                emit_qk(i + 1)
            emit_exp_pv(i)
            if t == NT - 1:
                emit_chunk_norm(b, c)
                if c == 3:
                    deferred.extend(make_oproj_groups(b))
                for _ in range(2):
                    if deferred:
                        deferred.pop(0)()
        for fn in deferred:
            fn()

    for layer in range(n_layers):
        layer_body(layer)
    P.barrier()
    P.emit()
    return nc, stack


_PROG_CACHE = {}


def _host_inputs(x_b, meta_tokens, norm_g, w_in, w_fmix, b_fmix, q_norm_g, k_norm_g, w_out, final_g):
    c = _constants()
    f32 = np.float32
    ng = np.ascontiguousarray(norm_g.reshape(DEPTH, KC, 128).transpose(2, 0, 1).reshape(128, DEPTH * KC)).astype(f32)
    swap = np.arange(64).reshape(32, 2)[:, ::-1].reshape(64)
    d_of_p = np.arange(128) % 64
    qkg = np.empty((128, DEPTH, 4), f32)
    qkg[:, :, 0] = q_norm_g[:, d_of_p].T
    qkg[:, :, 1] = q_norm_g[:, swap[d_of_p]].T
    qkg[:, :, 2] = k_norm_g[:, d_of_p].T
    qkg[:, :, 3] = k_norm_g[:, swap[d_of_p]].T
    bfm = np.ascontiguousarray(b_fmix.reshape(DEPTH, 4, 128).transpose(2, 0, 1).reshape(128, DEPTH * 4)).astype(f32)
    fg = np.ascontiguousarray(np.broadcast_to(final_g.reshape(1, D), (128, D))).astype(f32)
    return {
        "x": np.ascontiguousarray(x_b, dtype=f32),
        "meta": np.ascontiguousarray(meta_tokens, dtype=f32),
        "w_in": np.ascontiguousarray(w_in.reshape(DEPTH, KC, 128, 2304), dtype=f32),
        "w_out": np.ascontiguousarray(w_out, dtype=f32),
        "w_fmix": np.ascontiguousarray(w_fmix.reshape(DEPTH, 4, 128, 64), dtype=f32),
        "ng": ng, "qkg": np.ascontiguousarray(qkg.reshape(128, DEPTH * 4)), "bfm": bfm, "fg": fg,
        "cosT": c["cosT"], "sinT": c["sinT"], "m1": c["m1"], "c3": c["c3"], "cs64": c["cs64"],
        "ident": c["ident"], "bones": c["bones"],
    }


def kernel(x, meta_tokens, norm_g, w_in, w_fmix, b_fmix, q_norm_g, k_norm_g, w_out, final_g):
    x = np.asarray(x)
    args = [np.asarray(a) for a in (meta_tokens, norm_g, w_in, w_fmix, b_fmix, q_norm_g, k_norm_g, w_out, final_g)]
    nb = x.shape[0]
    if "full" not in _PROG_CACHE:
        _PROG_CACHE["full"] = build_program(DEPTH)
    nc, _stack = _PROG_CACHE["full"]
    in_maps = [_host_inputs(x[i], *args) for i in range(nb)]
    res = run_bass_kernel_spmd(nc, in_maps, core_ids=list(range(nb)))
    out = np.stack([np.asarray(r["out"]) for r in res.results], axis=0)
    return out.astype(np.float32)
```

```python
import contextlib
import math

import numpy as np
import ml_dtypes

import concourse.bass as bass
import concourse.mybir as mybir
from concourse.bass_utils import run_bass_kernel_spmd

F32 = mybir.dt.float32
BF16 = mybir.dt.bfloat16
AF = mybir.ActivationFunctionType
ALU = mybir.AluOpType

D = 1024
SEQ = 8192
NMETA = 16
L = SEQ + NMETA
NT = 65
KC = 8
DEPTH = 4
N1, N2 = 72, 114
G1 = 6
G3 = 4
EPS = 1e-6
C_FA, C_FB, C_FG, C_AG, C_Q, C_QP, C_K, C_KP, C_V, WCOLS = 0, 512, 1024, 1536, 2048, 2560, 3072, 3200, 3328, 3456

ENGS = ("pe", "act", "dve", "pool", "sp")


class Sem:
    def __init__(self, nc, stack, name):
        self.h = stack.enter_context(nc.semaphore(name))
        self.n = 0
        self.name = name


class Buf:
    __slots__ = ("name", "w", "r", "sem")

    def __init__(self, name, sem=None):
        self.name = name
        self.w = None
        self.r = []
        self.sem = sem


class Prog:
    def __init__(self, nc, stack):
        self.nc = nc
        self.stack = stack
        self.streams = {e: [] for e in ENGS}
        self.sems = []
        self.sem_by_name = {}
        self.esem = {}
        for e in ENGS:
            self.esem[e] = self.new_sem("e_" + e)

    def new_sem(self, name):
        if name in self.sem_by_name:
            return self.sem_by_name[name]
        s = Sem(self.nc, self.stack, name)
        self.sems.append(s)
        self.sem_by_name[name] = s
        return s

    def buf(self, name, dma=False):
        return Buf(name, self.new_sem("b_" + name) if dma else None)

    def _deps(self, eng, reads, writes):
        toks = []
        for b in reads:
            if b.w is not None:
                toks.append(b.w)
        for b in writes:
            if b.w is not None and b.w[2] != eng:
                toks.append(b.w)
            for t in b.r:
                if t[2] != eng:
                    toks.append(t)
        return toks

    def op(self, eng, fn, reads=(), writes=()):
        toks = self._deps(eng, reads, writes)
        sem = self.esem[eng]
        sem.n += 1
        tok = (sem, sem.n, eng)
        self.streams[eng].append((toks, fn, sem, 1))
        for b in reads:
            b.r.append(tok)
        for b in writes:
            b.w = tok
            b.r = []
        return tok

    def dma(self, eng, fn, reads=(), writes=(), ndma=1):
        wb = writes[0]
        assert wb.sem is not None, wb.name
        toks = [t for t in self._deps(None, reads, writes) if t[0] is not wb.sem]
        wb.sem.n += 16 * ndma
        tok = (wb.sem, wb.sem.n, "dma")
        self.streams[eng].append((toks, fn, wb.sem, 16))
        for b in reads:
            b.r.append(tok)
        for b in writes:
            b.w = tok
            b.r = []
        return tok

    def barrier(self):
        toks = [(s, s.n, "x") for s in self.sems if s.n > 0]
        for e in ENGS:
            self.streams[e].append((list(toks), None, None, 0))

    def emit(self):
        nc = self.nc
        with nc.Block() as block:
            def run(eng_name):
                def body(e):
                    seen = {}
                    for toks, fn, sem, amt in self.streams[eng_name]:
                        need = {}
                        for (s, v, _) in toks:
                            if seen.get(id(s), 0) >= v:
                                continue
                            if need.get(id(s), (None, 0))[1] < v:
                                need[id(s)] = (s, v)
                        for s, v in need.values():
                            e.wait_ge(s.h, v)
                            seen[id(s)] = v
                        if fn is None:
                            continue
                        r = fn(e)
                        if isinstance(r, (list, tuple)):
                            for ins in r:
                                ins.then_inc(sem.h, amt)
                        else:
                            r.then_inc(sem.h, amt)
                return body
            block.tensor(run("pe"))
            block.scalar(run("act"))
            block.vector(run("dve"))
            block.gpsimd(run("pool"))
            block.sync(run("sp"))


_CONST_CACHE = {}


def _constants():
    if _CONST_CACHE:
        return _CONST_CACHE
    f32 = np.float32
    rows = SEQ // 64
    j = np.arange(SEQ, dtype=np.int32)
    row = (j // 64 - rows // 2).astype(f32)
    col = (j % 64 - 32).astype(f32)
    freqs = (1.0 / (np.float32(10000.0) ** (np.arange(16, dtype=f32) * f32(2.0) / f32(32)))).astype(f32)
    ang = np.concatenate([row[:, None] * freqs, col[:, None] * freqs], axis=-1).astype(f32)
    ang = np.concatenate([np.zeros((NMETA, 32), f32), ang], axis=0)
    cos = np.cos(ang).astype(f32)
    sin = np.sin(ang).astype(f32)
    pair = (np.arange(128) % 64) // 2
    cosT = np.ascontiguousarray(cos[:, pair].T)
    sinT = np.ascontiguousarray(sin[:, pair].T)
    l1 = np.arange(N1, dtype=np.float64)[:, None, None]
    l2 = np.arange(N2, dtype=np.float64)[None, :, None]
    k1 = np.arange(N1, dtype=np.float64)[None, None, :]
    th = 2.0 * np.pi * (l1 * k1 / N1 + l2 * k1 / L)
    m1 = np.concatenate([np.cos(th), np.sin(th), -np.sin(th)], axis=-1)
    m1 = m1.astype(ml_dtypes.bfloat16)
    a = np.arange(N2, dtype=np.float64)
    th3 = 2.0 * np.pi * np.outer(a, a) / N2
    sc = 1.0 / math.sqrt(L * 64.0)
    c3 = np.stack([np.cos(th3) * sc, -np.sin(th3) * sc], axis=1).astype(ml_dtypes.bfloat16)
    c = np.arange(64, dtype=np.float64)
    th64 = 2.0 * np.pi * np.outer(c, c) / 64.0
    cs = np.zeros((128, 2, 128), f32)
    for gl in range(2):
        cs[gl * 64:(gl + 1) * 64, 0, gl * 64:(gl + 1) * 64] = np.cos(th64)
        cs[gl * 64:(gl + 1) * 64, 1, gl * 64:(gl + 1) * 64] = np.sin(th64)
    ident = np.eye(128, dtype=f32)
    bones = np.zeros((128, 128), f32)
    bones[:64, :64] = 1.0
    bones[64:, 64:] = 1.0
    _CONST_CACHE.update(cosT=cosT, sinT=sinT, m1=m1, c3=c3, cs64=cs, ident=ident, bones=bones)
    return _CONST_CACHE


def build_program(n_layers=DEPTH, debug=False, final_norm=True):
    nc = bass.Bass("TRN2", target_bir_lowering=False)
    stack = contextlib.ExitStack()

    def din(name, shape, dt=F32):
        return nc.dram_tensor(name, list(shape), dt, kind="ExternalInput").ap()

    def dscr(name, shape, dt):
        return nc.dram_tensor(name, list(shape), dt, kind="ExternalOutput" if debug else "Internal").ap()

    x_d = din("x", [SEQ, D])
    meta_d = din("meta", [NMETA, D])
    win_d = din("w_in", [DEPTH, KC, 128, 2304])
    wout_d = din("w_out", [DEPTH, 1024, 1024])
    wfm_d = din("w_fmix", [DEPTH, 4, 128, 64])
    ng_d = din("ng", [128, DEPTH * KC])
    qkg_d = din("qkg", [128, DEPTH * 4])
    bfm_d = din("bfm", [128, DEPTH * 4])
    fg_d = din("fg", [128, D])
    cos_d = din("cosT", [128, L])
    sin_d = din("sinT", [128, L])
    m1_d = din("m1", [N1, N2, 216], BF16)
    c3_d = din("c3", [N2, 2, N2], BF16)
    cs64_d = din("cs64", [128, 2, 128])
    ident_d = din("ident", [128, 128])
    bones_d = din("bones", [128, 128])
    out_d = nc.dram_tensor("out", [SEQ, D], F32, kind="ExternalOutput").ap()

    h_d = [dscr("hA", [L, D], F32), dscr("hB", [L, D], F32)]
    z_d = dscr("zs", [L, 1024], BF16)
    b_d = dscr("bs", [N1, N2, 2, 512], BF16)
    gf_d = dscr("gfs", [4, 128, L], BF16)
    ga_d = dscr("gas", [4, 128, L], BF16)
    qt_d = dscr("qts", [4, 128, L], BF16)
    mas_d = dscr("mas", [4, 128, L], BF16) if debug else None

    ARENA_KB = 200
    arena = stack.enter_context(nc.sbuf_tensor("arena", [128, ARENA_KB * 256], F32))
    psum = stack.enter_context(nc.psum_tensor("psum", [128, 7, 512], F32))
    psb = stack.enter_context(nc.psum_tensor("psb", [128, 1024], BF16))

    P = Prog(nc, stack)

    class Carver:
        def __init__(self, base):
            self.off = base

        def take(self, shape, dt):
            esz = 4 if dt == F32 else 2
            n = 1
            for s in shape[1:]:
                n *= s
            nbytes = (n * esz + 3) // 4 * 4
            o4 = self.off // 4
            v = arena[:, o4:o4 + nbytes // 4]
            if dt != F32:
                v = v.bitcast(dt)
            v = v[0:shape[0], 0:n]
            if len(shape) == 3:
                v = v.rearrange("p (a b) -> p a b", b=shape[2])
            elif len(shape) == 4:
                v = v.rearrange("p (a b c) -> p a b c", b=shape[2], c=shape[3])
            self.off += nbytes
            assert self.off <= ARENA_KB * 1024, self.off
            return v

    cv = Carver(0)
    KT = cv.take([128, L], BF16)
    VX = cv.take([128, NT, 192], BF16)
    IDB = cv.take([128, 128], BF16)
    IDF = cv.take([128, 128], F32)
    BONES = cv.take([128, 128], F32)
    CS64 = cv.take([128, 2, 128], F32)
    NG = cv.take([128, DEPTH * KC], F32)
    QKG = cv.take([128, DEPTH * 4], F32)
    BFM = cv.take([128, DEPTH * 4], F32)
    FG = cv.take([128, D], F32)
    EPSQ = cv.take([128, 1], F32)
    EPSN = cv.take([128, 1], F32)
    STAT = cv.take([128, 8], F32)
    PHASE_BASE = cv.off

    b_KT = P.buf("KT")
    b_VX = P.buf("VX")
    b_const = P.buf("const", dma=True)
    b_h = [P.buf("hA", dma=True), P.buf("hB", dma=True)]
    b_z = P.buf("zs", dma=True)
    b_bs = P.buf("bs", dma=True)
    b_gf = P.buf("gfs", dma=True)
    b_ga = P.buf("gas", dma=True)
    b_qt = P.buf("qts", dma=True)
    b_out = P.buf("out", dma=True)
    b_mas = P.buf("mas", dma=True)
    b_ps = [P.buf("ps%d" % i) for i in range(7)]
    b_psb = P.buf("psb")
    b_idb = P.buf("idb")
    b_stat = P.buf("stat")

    def ld_consts(e):
        return [
            e.dma_start(out=IDF, in_=ident_d),
            e.dma_start(out=BONES, in_=bones_d),
            e.dma_start(out=CS64, in_=cs64_d),
            e.dma_start(out=NG, in_=ng_d),
            e.dma_start(out=QKG, in_=qkg_d),
            e.dma_start(out=BFM, in_=bfm_d),
            e.dma_start(out=FG, in_=fg_d),
        ]
    P.dma("sp", ld_consts, writes=[b_const], ndma=7)

    def init_h(e):
        return [
            e.dma_start(out=h_d[0][0:NMETA, :], in_=meta_d),
            e.dma_start(out=h_d[0][NMETA:NMETA + 4096, :], in_=x_d[0:4096, :]),
            e.dma_start(out=h_d[0][NMETA + 4096:L, :], in_=x_d[4096:SEQ, :]),
        ]
    P.dma("pool", init_h, writes=[b_h[0]], ndma=3)

    def init_dve(e):
        e.memset(EPSQ, EPS)
        e.memset(EPSN, EPS)
        e.memset(VX[:, :, 64:128], 1.0)
        return e.tensor_copy(IDB, IDF)
    P.op("dve", init_dve, reads=[b_const], writes=[b_idb, b_VX, b_stat])

    def tiles_of_block(b):
        if b < 16:
            return [(4 * b + i, 512 * b + 128 * i, 128) for i in range(4)]
        return [(64, 8192, 16)]

    NBLK = 17

    def layer_body(layer):
        hin, hout = h_d[layer % 2], h_d[(layer + 1) % 2]
        b_hin, b_hout = b_h[layer % 2], b_h[(layer + 1) % 2]
        last = (layer == n_layers - 1)

        P.barrier()
        cv = Carver(PHASE_BASE)
        WB = cv.take([128, KC, WCOLS], BF16)
        P1_BASE = cv.off
        WS = [cv.take([128, 2304], F32) for _ in range(2)]
        WT = [cv.take([128, 128], F32) for _ in range(2)]
        GB = cv.take([128, 4, 2, 128], F32)
        WFM = cv.take([128, 4, 64], F32)
        b_WB = P.buf("WB")
        b_WS = [P.buf("WS0", dma=True), P.buf("WS1", dma=True)]
        b_WT = [P.buf("WT0"), P.buf("WT1")]
        b_GB = P.buf("GB")
        b_WFM = P.buf("WFM", dma=True)

        P.dma("sp", lambda e: e.dma_start(out=WFM, in_=wfm_d[layer].rearrange("j p d -> p j d")),
              writes=[b_WFM])
        P.op("dve", lambda e: e.memset(GB, 0.0), writes=[b_GB])
        for j in range(4):
            for cs in range(2):
                pb = (2 * j + cs) % 4
                P.op("pe", lambda e, j=j, cs=cs, pb=pb: e.matmul(
                    psum[:, pb, 0:64], CS64[:, cs, :], WFM[:, j, :], start=True, stop=True),
                    reads=[b_const, b_WFM], writes=[b_ps[pb]])

                def evac_g(e, j=j, cs=cs, pb=pb):
                    e.tensor_copy(GB[0:64, j, cs, 0:64], psum[0:64, pb, 0:64])
                    return e.tensor_copy(GB[64:128, j, cs, 64:128], psum[64:128, pb, 0:64])
                P.op("dve", evac_g, reads=[b_ps[pb]], writes=[b_GB])

        qg_i = 4 * layer
        for kc in range(KC):
            s = kc % 2
            ws = WS[s]
            P.dma("sp", lambda e, kc=kc, ws=ws: e.dma_start(out=ws, in_=win_d[layer, kc]), writes=[b_WS[s]])
            gn = NG[:, layer * KC + kc: layer * KC + kc + 1]

            def cast_cols(e, kc=kc, ws=ws, gn=gn):
                wb = WB[:, kc, :]
                e.tensor_scalar_mul(wb[:, C_FG:C_FG + 512], ws[:, 512:1024], gn)
                for (dst, src) in ((C_AG, 1792), (C_Q, 1024)):
                    e.tensor_scalar_mul(
                        wb[:, dst:dst + 512].rearrange("p (c h d) -> p c h d", c=4, h=2, d=64),
                        ws[:, src:src + 512].rearrange("p (h c d) -> p c h d", h=2, c=4, d=64), gn)
                for half in range(2):
                    srcv = ws[:, 1024 + half * 256: 1024 + (half + 1) * 256].rearrange(
                        "p (c i t) -> p c i t", c=4, i=32, t=2)
                    dstv = wb[:, C_QP:C_QP + 512].rearrange("p (c h i t) -> p c h i t", c=4, h=2, i=32, t=2)[:, :, half]
                    e.tensor_scalar(dstv[:, :, :, 0], srcv[:, :, :, 1], gn, -1.0, ALU.mult, ALU.mult)
                    e.tensor_scalar_mul(dstv[:, :, :, 1], srcv[:, :, :, 0], gn)
                e.tensor_scalar_mul(wb[:, C_K:C_K + 128], ws[:, 1536:1664], gn)
                ksv = ws[:, 1536:1664].rearrange("p (i t) -> p i t", t=2)
                kdv = wb[:, C_KP:C_KP + 128].rearrange("p (i t) -> p i t", t=2)
                e.tensor_scalar(kdv[:, :, 0], ksv[:, :, 1], gn, -1.0, ALU.mult, ALU.mult)
                e.tensor_scalar_mul(kdv[:, :, 1], ksv[:, :, 0], gn)
                return e.tensor_scalar_mul(wb[:, C_V:C_V + 128], ws[:, 1664:1792], gn)
            P.op("dve", cast_cols, reads=[b_WS[s], b_const], writes=[b_WB])

            for j in range(4):
                t = (kc * 4 + j) % 2
                pb = (kc * 4 + j) % 2
                pm = 2 + (kc * 4 + j) % 2
                P.op("pe", lambda e, ws=ws, j=j, pb=pb: e.transpose(
                    psum[:, pb, 0:128], ws[:, j * 128:(j + 1) * 128], IDF),
                    reads=[b_WS[s], b_const], writes=[b_ps[pb]])
                P.op("dve", lambda e, t=t, pb=pb: e.tensor_copy(WT[t], psum[:, pb, 0:128]),
                     reads=[b_ps[pb]], writes=[b_WT[t]])
                P.op("pe", lambda e, t=t, j=j, pm=pm: e.matmul(
                    psum[:, pm, 0:256], WT[t], GB[:, j].rearrange("p a b -> p (a b)"), start=True, stop=True),
                    reads=[b_WT[t], b_GB], writes=[b_ps[pm]])

                def evac_w(e, kc=kc, j=j, pm=pm, gn=gn):
                    e.tensor_scalar_mul(WB[:, kc, C_FA + j * 128: C_FA + (j + 1) * 128], psum[:, pm, 0:128], gn)
                    return e.tensor_scalar_mul(WB[:, kc, C_FB + j * 128: C_FB + (j + 1) * 128], psum[:, pm, 128:256], gn)
                P.op("dve", evac_w, reads=[b_ps[pm], b_const], writes=[b_WB])

        P.barrier()
        cv = Carver(P1_BASE)
        HT = [cv.take([128, D], F32) for _ in range(2)]
        SQJ = cv.take([128, D], BF16)
        U = [cv.take([128, D], BF16) for _ in range(2)]
        UT = [cv.take([128, KC, 512], BF16) for _ in range(2)]
        ZS = [cv.take([128, 1024], BF16) for _ in range(2)]
        GS = [cv.take([128, 4, 512], BF16) for _ in range(2)]
        QS = [cv.take([128, 4, 512], BF16) for _ in range(2)]
        COS = [cv.take([128, 512], F32) for _ in range(2)]
        SIN = [cv.take([128, 512], F32) for _ in range(2)]
        SQ = [cv.take([128, 512], F32) for _ in range(2)]
        RSTD = [cv.take([128, 512], F32) for _ in range(2)]
        T1 = [cv.take([128, 512], F32) for _ in range(2)]
        T2 = [cv.take([128, 512], F32) for _ in range(2)]
        b_HT = [P.buf("HT%d" % i, dma=True) for i in range(2)]
        b_SQJ = P.buf("SQJ")
        b_U = [P.buf("U%d" % i) for i in range(2)]
        b_UT = [P.buf("UT%d" % i) for i in range(2)]
        b_ZS = [P.buf("ZS%d" % i) for i in range(2)]
        b_GS = [P.buf("GS%d" % i) for i in range(2)]
        b_QS = [P.buf("QS%d" % i) for i in range(2)]
        b_CS = [P.buf("CS%d" % i, dma=True) for i in range(2)]
        b_SQ = [P.buf("SQ%d" % i) for i in range(2)]
        b_RSTD = [P.buf("RSTD%d" % i) for i in range(2)]
        b_T1 = [P.buf("T1%d" % i) for i in range(2)]
        b_T2 = [P.buf("T2%d" % i) for i in range(2)]
        b_ss = [P.buf("ss%d" % i) for i in range(2)]

        tcount_l = [0]

        def front_tables(bb):
            tl = tiles_of_block(bb)
            nb = sum(r for (_, _, r) in tl)
            tok0 = tl[0][1]
            cs_s = bb % 2

            def ld_cs(e):
                return [e.dma_start(out=COS[cs_s][:, 0:nb], in_=cos_d[:, tok0:tok0 + nb]),
                        e.dma_start(out=SIN[cs_s][:, 0:nb], in_=sin_d[:, tok0:tok0 + nb])]
            P.dma("sp", ld_cs, writes=[b_CS[cs_s]], ndma=2)

        def front_tile(bb, idx):
            tl = tiles_of_block(bb)
            tok0 = tl[0][1]
            us = bb % 2
            (ti, t0, tr) = tl[idx]
            hs = tcount_l[0] % 2
            tcount_l[0] += 1
            loc = t0 - tok0
            P.dma("sp", lambda e: e.dma_start(out=HT[hs][0:tr, :], in_=hin[t0:t0 + tr, :]),
                  reads=[b_hin], writes=[b_HT[hs]])
            P.op("act", lambda e: e.activation(
                out=SQJ[0:tr, :], in_=HT[hs][0:tr, :], func=AF.Square, accum_out=STAT[0:tr, hs:hs + 1]),
                reads=[b_HT[hs]], writes=[b_SQJ, b_ss[hs]])
            P.op("act", lambda e: e.activation(
                out=STAT[0:tr, 2 + hs:3 + hs], in_=STAT[0:tr, hs:hs + 1], func=AF.Ln, bias=EPSN[0:tr, :], scale=1.0 / D),
                reads=[b_ss[hs], b_stat], writes=[b_ss[hs]])
            P.op("act", lambda e: e.activation(
                out=STAT[0:tr, 4 + hs:5 + hs], in_=STAT[0:tr, 2 + hs:3 + hs], func=AF.Exp, scale=-0.5),
                reads=[b_ss[hs]], writes=[b_ss[hs]])
            P.op("dve", lambda e: e.tensor_scalar_mul(
                U[hs][0:tr, :], HT[hs][0:tr, :], STAT[0:tr, 4 + hs:5 + hs]),
                reads=[b_HT[hs], b_ss[hs]], writes=[b_U[hs]])

            def tr_u(e):
                r = None
                for kc in range(KC):
                    r = e.transpose(psb[:, kc * 128: kc * 128 + tr], U[hs][0:tr, kc * 128:(kc + 1) * 128], IDB[0:tr, 0:tr])
                return r
            P.op("pe", tr_u, reads=[b_U[hs], b_idb], writes=[b_psb])
            P.op("dve", lambda e: e.tensor_copy(
                UT[us][:, :, loc:loc + tr], psb[:, :].rearrange("p (k t) -> p k t", t=128)[:, :, 0:tr]),
                reads=[b_psb], writes=[b_UT[us]])

        front_tables(0)
        for _idx in range(len(tiles_of_block(0))):
            front_tile(0, _idx)
        tcount = 0
        mmc = 0
        gcount = 0
        qcount = 0
        for b in range(NBLK):
            tl = tiles_of_block(b)
            nb = sum(r for (_, _, r) in tl)
            tok0 = tl[0][1]
            us = b % 2
            cs_s = b % 2
            for (ti, t0, tr) in tl:
                loc = t0 - tok0
                zs = ti % 2
                for half in range(2):
                    pb = mmc % 3
                    mmc += 1

                    def mm_z(e, us=us, loc=loc, tr=tr, half=half, pb=pb):
                        r = None
                        for kc in range(KC):
                            r = e.matmul(psum[0:tr, pb, :], UT[us][:, kc, loc:loc + tr],
                                         WB[:, kc, half * 512:(half + 1) * 512], start=(kc == 0), stop=(kc == KC - 1))
                        return r
                    P.op("pe", mm_z, reads=[b_UT[us], b_WB], writes=[b_ps[pb]])
                    P.op("dve", lambda e, zs=zs, tr=tr, half=half, pb=pb: e.tensor_copy(
                        ZS[zs][0:tr, half * 512:(half + 1) * 512], psum[0:tr, pb, :]),
                        reads=[b_ps[pb]], writes=[b_ZS[zs]])
                P.dma("pool", lambda e, zs=zs, t0=t0, tr=tr: e.dma_start(out=z_d[t0:t0 + tr, :], in_=ZS[zs][0:tr, :]),
                      reads=[b_ZS[zs]], writes=[b_z])
                pb = mmc % 3
                mmc += 1

                def mm_v(e, us=us, loc=loc, tr=tr, pb=pb):
                    r = None
                    for kc in range(KC):
                        r = e.matmul(psum[0:tr, pb, 0:128], UT[us][:, kc, loc:loc + tr],
                                     WB[:, kc, C_V:C_V + 128], start=(kc == 0), stop=(kc == KC - 1))
                    return r
                P.op("pe", mm_v, reads=[b_UT[us], b_WB], writes=[b_ps[pb]])

                def ev_v(e, ti=ti, tr=tr, pb=pb):
                    e.tensor_copy(VX[0:tr, ti, 0:64], psum[0:tr, pb, 0:64])
                    return e.tensor_copy(VX[0:tr, ti, 128:192], psum[0:tr, pb, 64:128])
                P.op("dve", ev_v, reads=[b_ps[pb]], writes=[b_VX])
                _k = ti - tl[0][0]
                if b + 1 < NBLK:
                    if _k == 0:
                        front_tables(b + 1)
                    if _k < len(tiles_of_block(b + 1)):
                        front_tile(b + 1, _k)

            for gi, (col0, dst_d, b_dst) in enumerate(((C_FG, gf_d, b_gf), (C_AG, ga_d, b_ga))):
                gs = gcount % 2
                gcount += 1
                for j in range(4):
                    pb = mmc % 3
                    mmc += 1

                    def mm_g(e, us=us, nb=nb, col=col0 + j * 128, pb=pb):
                        r = None
                        for kc in range(KC):
                            r = e.matmul(psum[:, pb, 0:nb], WB[:, kc, col:col + 128], UT[us][:, kc, 0:nb],
                                         start=(kc == 0), stop=(kc == KC - 1))
                        return r
                    P.op("pe", mm_g, reads=[b_UT[us], b_WB], writes=[b_ps[pb]])
                    P.op("act", lambda e, gs=gs, j=j, nb=nb, pb=pb: e.activation(
                        out=GS[gs][:, j, 0:nb], in_=psum[:, pb, 0:nb], func=AF.Silu),
                        reads=[b_ps[pb]], writes=[b_GS[gs]])
                P.dma("pool", lambda e, gs=gs, dst_d=dst_d, tok0=tok0, nb=nb: e.dma_start(
                    out=dst_d[:, :, tok0:tok0 + nb].rearrange("c p t -> p c t"), in_=GS[gs][:, :, 0:nb]),
                    reads=[b_GS[gs]], writes=[b_dst])

            qs = b % 2
            for ci in range(5):
                isk = (ci == 4)
                col = C_K if isk else C_Q + ci * 128
                colp = C_KP if isk else C_QP + ci * 128
                gcol = qg_i + (2 if isk else 0)
                w = qcount % 2
                qa = 3 + 2 * (qcount % 2)
                qb = qa + 1
                qcount += 1
                pss = mmc % 3
                mmc += 1

                def mm_q(e, us=us, nb=nb, col=col, colp=colp, qa=qa, qb=qb):
                    r = None
                    for kc in range(KC):
                        e.matmul(psum[:, qa, 0:nb], WB[:, kc, col:col + 128], UT[us][:, kc, 0:nb],
                                 start=(kc == 0), stop=(kc == KC - 1))
                    for kc in range(KC):
                        r = e.matmul(psum[:, qb, 0:nb], WB[:, kc, colp:colp + 128], UT[us][:, kc, 0:nb],
                                     start=(kc == 0), stop=(kc == KC - 1))
                    return r
                P.op("pe", mm_q, reads=[b_UT[us], b_WB], writes=[b_ps[qa], b_ps[qb]])
                P.op("act", lambda e, w=w, nb=nb, qa=qa: e.activation(out=SQ[w][:, 0:nb], in_=psum[:, qa, 0:nb], func=AF.Square),
                     reads=[b_ps[qa]], writes=[b_SQ[w]])
                P.op("pe", lambda e, w=w, nb=nb, pss=pss: e.matmul(psum[:, pss, 0:nb], BONES, SQ[w][:, 0:nb], start=True, stop=True),
                     reads=[b_SQ[w], b_const], writes=[b_ps[pss]])
                P.op("act", lambda e, w=w, nb=nb, pss=pss: e.activation(
                    out=SQ[w][:, 0:nb], in_=psum[:, pss, 0:nb], func=AF.Ln, bias=EPSQ, scale=1.0 / 64),
                    reads=[b_ps[pss], b_stat], writes=[b_SQ[w]])
                P.op("act", lambda e, w=w, nb=nb: e.activation(
                    out=RSTD[w][:, 0:nb], in_=SQ[w][:, 0:nb], func=AF.Exp, scale=-0.5),
                    reads=[b_SQ[w]], writes=[b_RSTD[w]])
                P.op("dve", lambda e, w=w, nb=nb, gcol=gcol, qa=qa: e.scalar_tensor_tensor(
                    out=T1[w][:, 0:nb], in0=psum[:, qa, 0:nb], scalar=QKG[:, gcol:gcol + 1], in1=RSTD[w][:, 0:nb],
                    op0=ALU.mult, op1=ALU.mult),
                    reads=[b_ps[qa], b_RSTD[w], b_const], writes=[b_T1[w]])
                P.op("dve", lambda e, w=w, nb=nb, gcol=gcol, qb=qb: e.scalar_tensor_tensor(
                    out=T2[w][:, 0:nb], in0=psum[:, qb, 0:nb], scalar=QKG[:, gcol + 1:gcol + 2], in1=RSTD[w][:, 0:nb],
                    op0=ALU.mult, op1=ALU.mult),
                    reads=[b_ps[qb], b_RSTD[w], b_const], writes=[b_T2[w]])
                P.op("dve", lambda e, w=w, nb=nb, cs_s=cs_s: e.tensor_tensor(
                    T1[w][:, 0:nb], T1[w][:, 0:nb], COS[cs_s][:, 0:nb], ALU.mult),
                    reads=[b_T1[w], b_CS[cs_s]], writes=[b_T1[w]])
                P.op("dve", lambda e, w=w, nb=nb, cs_s=cs_s: e.tensor_tensor(
                    T2[w][:, 0:nb], T2[w][:, 0:nb], SIN[cs_s][:, 0:nb], ALU.mult),
                    reads=[b_T2[w], b_CS[cs_s]], writes=[b_T2[w]])
                if isk:
                    P.op("dve", lambda e, w=w, nb=nb, tok0=tok0: e.tensor_tensor(
                        KT[:, tok0:tok0 + nb], T1[w][:, 0:nb], T2[w][:, 0:nb], ALU.add),
                        reads=[b_T1[w], b_T2[w]], writes=[b_KT])
                else:
                    P.op("dve", lambda e, w=w, nb=nb, qs=qs, ci=ci: e.tensor_tensor(
                        QS[qs][:, ci, 0:nb], T1[w][:, 0:nb], T2[w][:, 0:nb], ALU.add),
                        reads=[b_T1[w], b_T2[w]], writes=[b_QS[qs]])
            P.dma("pool", lambda e, qs=qs, tok0=tok0, nb=nb: e.dma_start(
                out=qt_d[:, :, tok0:tok0 + nb].rearrange("c p t -> p c t"), in_=QS[qs][:, :, 0:nb]),
                reads=[b_QS[qs]], writes=[b_qt])

        P.barrier()
        cv = Carver(PHASE_BASE)
        GF = cv.take([128, 4, L], BF16)
        ZT = [cv.take([N1, G1, 1024], BF16) for _ in range(2)]
        M1 = [cv.take([N1, G1, 216], BF16) for _ in range(2)]
        BT = [cv.take([N1, G1, 2, 512], BF16) for _ in range(2)]
        BL = [cv.take([N2, G3, 1024], BF16) for _ in range(2)]
        C3 = cv.take([N2, 2, N2], BF16)
        b_GF = P.buf("GF", dma=True)
        b_ZT = [P.buf("ZT%d" % i, dma=True) for i in range(2)]
        b_BT = [P.buf("BT%d" % i) for i in range(2)]
        b_BL = [P.buf("BL%d" % i, dma=True) for i in range(2)]
        b_C3 = P.buf("C3", dma=True)

        P.dma("sp", lambda e: e.dma_start(out=C3, in_=c3_d), writes=[b_C3])
        P.dma("sp", lambda e: [e.dma_start(out=GF[:, c, :], in_=gf_d[c]) for c in range(4)],
              reads=[b_gf], writes=[b_GF], ndma=4)

        z_v = z_d.rearrange("(a b) c -> a b c", b=N2)
        for gi in range(N2 // G1):
            s = gi % 2

            def ld_z(e, s=s, gi=gi):
                return [e.dma_start(out=ZT[s], in_=z_v[:, gi * G1:(gi + 1) * G1, :]),
                        e.dma_start(out=M1[s], in_=m1_d[:, gi * G1:(gi + 1) * G1, :])]
            P.dma("sp", ld_z, reads=[b_z], writes=[b_ZT[s]], ndma=2)
            for g in range(G1):
                pr = (gi * G1 + g) % 2
                pi = 2 + (gi * G1 + g) % 2

                def mm_s1(e, s=s, g=g, pr=pr, pi=pi):
                    fa = ZT[s][:, g, 0:512]
                    fb = ZT[s][:, g, 512:1024]
                    e.matmul(psum[0:N1, pr, :], M1[s][:, g, 0:72], fa, start=True, stop=False)
                    e.matmul(psum[0:N1, pr, :], M1[s][:, g, 144:216], fb, start=False, stop=True)
                    e.matmul(psum[0:N1, pi, :], M1[s][:, g, 72:144], fa, start=True, stop=False)
                    return e.matmul(psum[0:N1, pi, :], M1[s][:, g, 0:72], fb, start=False, stop=True)
                P.op("pe", mm_s1, reads=[b_ZT[s]], writes=[b_ps[pr], b_ps[pi]])
                P.op("dve", lambda e, s=s, g=g, pr=pr: e.tensor_copy(BT[s][:, g, 0, :], psum[0:N1, pr, :]),
                     reads=[b_ps[pr]], writes=[b_BT[s]])
                P.op("act", lambda e, s=s, g=g, pi=pi: e.activation(out=BT[s][:, g, 1, :], in_=psum[0:N1, pi, :], func=AF.Copy),
                     reads=[b_ps[pi]], writes=[b_BT[s]])
            P.dma("pool", lambda e, s=s, gi=gi: e.dma_start(out=b_d[:, gi * G1:(gi + 1) * G1, :, :], in_=BT[s]),
                  reads=[b_BT[s]], writes=[b_bs])

        b_v = b_d.rearrange("k l r c -> l k (r c)")
        poc = 0
        for kg in range(N1 // G3):
            s = kg % 2
            P.dma("sp", lambda e, s=s, kg=kg: e.dma_start(out=BL[s], in_=b_v[:, kg * G3:(kg + 1) * G3, :]),
                  reads=[b_bs], writes=[b_BL[s]])
            for c in range(4):
                pb = 4 + poc % 3
                poc += 1
                pov = psum[:, pb, 0:G3 * N2].rearrange("p (a k) -> p a k", k=N2)

                def mm_s3(e, s=s, c=c, pov=pov):
                    r = None
                    for a in range(G3):
                        e.matmul(pov[:, a, :], BL[s][:, a, c * 128:(c + 1) * 128], C3[:, 0, :], start=True, stop=False)
                        r = e.matmul(pov[:, a, :], BL[s][:, a, 512 + c * 128: 512 + (c + 1) * 128], C3[:, 1, :],
                                     start=False, stop=True)
                    return r
                P.op("pe", mm_s3, reads=[b_BL[s], b_C3], writes=[b_ps[pb]])
                gview = GF[:, c, :].rearrange("p (k2 k1) -> p k1 k2", k1=N1)[:, kg * G3:(kg + 1) * G3, :]
                P.op("dve", lambda e, c=c, pov=pov, gview=gview: e.scalar_tensor_tensor(
                    out=gview, in0=pov, scalar=BFM[:, layer * 4 + c: layer * 4 + c + 1], in1=gview,
                    op0=ALU.add, op1=ALU.mult),
                    reads=[b_ps[pb], b_GF, b_const], writes=[b_GF])
        P.dma("pool", lambda e: [e.dma_start(out=gf_d[c], in_=GF[:, c, :]) for c in range(4)],
              reads=[b_GF], writes=[b_gf], ndma=4)

        P.barrier()
        cv = Carver(PHASE_BASE)
        WO = cv.take([128, KC, D], BF16)
        WOS = [cv.take([128, D], F32) for _ in range(2)]
        QT = [cv.take([128, 4, 512], BF16) for _ in range(2)]
        GA = [cv.take([128, 4, 512], BF16) for _ in range(2)]
        GFB = [cv.take([128, 4, 512], BF16) for _ in range(3)]
        MA = [cv.take([128, 4, 512], BF16) for _ in range(2)]
        NPT = 3
        PT = [cv.take([128, 3, 512], BF16) for _ in range(NPT)]
        HT3 = [cv.take([128, D], F32) for _ in range(2)]
        HN = [cv.take([128, D], F32) for _ in range(2)]
        RC = [cv.take([128, 512], F32) for _ in range(2)]
        TMP = [cv.take([128, 512], F32) for _ in range(2)]
        SQJ3 = cv.take([128, D], BF16)
        b_WO = P.buf("WO")
        b_WOS = [P.buf("WOS%d" % i, dma=True) for i in range(2)]
        b_QT = [P.buf("QT%d" % i, dma=True) for i in range(2)]
        b_GA = [P.buf("GA%d" % i, dma=True) for i in range(2)]
        b_GFB = [P.buf("GFB%d" % i, dma=True) for i in range(3)]
        b_MA = [P.buf("MA%d" % i) for i in range(2)]
        b_PT = [P.buf("PT%d" % i) for i in range(NPT)]
        b_HT3 = [P.buf("HT3%d" % i, dma=True) for i in range(2)]
        b_HN = [P.buf("HN%d" % i) for i in range(2)]
        b_RC = [P.buf("RC%d" % i) for i in range(2)]
        b_TMP = [P.buf("TMP%d" % i) for i in range(2)]
        b_SQJ3 = P.buf("SQJ3")
        b_st3 = [P.buf("st3%d" % i) for i in range(2)]
        b_ST = [P.buf("STa"), P.buf("STb")]

        for kc in range(KC):
            s = kc % 2
            if kc < 4:
                P.dma("sp", lambda e, s=s, kc=kc: e.dma_start(out=WOS[s], in_=wout_d[layer, kc * 128:(kc + 1) * 128, :]),
                      writes=[b_WOS[s]])
            else:
                c = kc - 4

                def ld_wo(e, s=s, c=c):
                    return [e.dma_start(out=WOS[s][0:64, :], in_=wout_d[layer, 512 + 64 * c: 512 + 64 * (c + 1), :]),
                            e.dma_start(out=WOS[s][64:128, :], in_=wout_d[layer, 512 + 64 * (4 + c): 512 + 64 * (5 + c), :])]
                P.dma("sp", ld_wo, writes=[b_WOS[s]], ndma=2)
            P.op("dve", lambda e, s=s, kc=kc: e.tensor_copy(WO[:, kc, :], WOS[s]), reads=[b_WOS[s]], writes=[b_WO])

        blk = []
        for b in range(NBLK):
            tl = tiles_of_block(b)
            blk.append((tl, sum(r for (_, _, r) in tl), tl[0][1]))
        steps = []
        for b in range(NBLK):
            for c in range(4):
                for t in range(NT):
                    steps.append((b, c, t))
        nsteps = len(steps)
        b_O = [b_ps[4], b_ps[5]]
        OPB = 6

        def emit_block_loads(b):
            tl, nb, tok0 = blk[b]
            s = b % 2
            s3 = b % 3
            P.dma("sp", lambda e: e.dma_start(
                out=QT[s][:, :, 0:nb], in_=qt_d[:, :, tok0:tok0 + nb].rearrange("c p t -> p c t")),
                reads=[b_qt], writes=[b_QT[s]])
            P.dma("sp", lambda e: e.dma_start(
                out=GA[s][:, :, 0:nb], in_=ga_d[:, :, tok0:tok0 + nb].rearrange("c p t -> p c t")),
                reads=[b_ga], writes=[b_GA[s]])
            P.dma("sp", lambda e: e.dma_start(
                out=GFB[s3][:, :, 0:nb], in_=gf_d[:, :, tok0:tok0 + nb].rearrange("c p t -> p c t")),
                reads=[b_gf], writes=[b_GFB[s3]])

        def emit_qk(i):
            b, c, t = steps[i]
            tl, nb, tok0 = blk[b]
            s = b % 2
            sg = i % 2
            bank0 = 2 * sg
            kr = 128 if t < 64 else 16

            def mm_qk(e):
                e.matmul(psum[0:kr, bank0, 0:nb], KT[0:64, t * 128: t * 128 + kr], QT[s][0:64, c, 0:nb],
                         start=True, stop=True)
                return e.matmul(psum[0:kr, bank0 + 1, 0:nb], KT[64:128, t * 128: t * 128 + kr], QT[s][64:128, c, 0:nb],
                                start=True, stop=True)
            P.op("pe", mm_qk, reads=[b_KT, b_QT[s]], writes=[b_ST[sg]])

        def emit_exp_pv(i):
            b, c, t = steps[i]
            tl, nb, tok0 = blk[b]
            sg = i % 2
            pt = i % NPT
            bank0 = 2 * sg
            kr = 128 if t < 64 else 16
            P.op("act", lambda e: e.activation(out=PT[pt][0:kr, 0:2, 0:nb], in_=psum[0:kr, bank0:bank0 + 2, 0:nb],
                                               func=AF.Exp, scale=0.125),
                 reads=[b_ST[sg]], writes=[b_PT[pt]])

            def mm_pv(e):
                e.matmul(psum[:, 4, 0:nb], VX[0:kr, t, 0:128], PT[pt][0:kr, 0, 0:nb], start=(t == 0), stop=(t == NT - 1))
                return e.matmul(psum[:, 5, 0:nb], VX[0:kr, t, 64:192], PT[pt][0:kr, 1, 0:nb],
                                start=(t == 0), stop=(t == NT - 1))
            P.op("pe", mm_pv, reads=[b_VX, b_PT[pt]], writes=[b_O[0], b_O[1]])

        def emit_chunk_norm(b, c):
            tl, nb, tok0 = blk[b]
            s = b % 2
            for half in range(2):
                r0 = half * 64
                nr = slice(r0, r0 + 64)
                dr = slice(64 - r0, 128 - r0)
                ob = 4 + half
                P.op("act", lambda e, dr=dr, ob=ob, half=half: e.activation(
                    out=RC[half][dr, 0:nb], in_=psum[dr, ob, 0:nb], func=AF.Ln),
                    reads=[b_O[half]], writes=[b_RC[half]])
                P.op("act", lambda e, dr=dr, half=half: e.activation(
                    out=RC[half][dr, 0:nb], in_=RC[half][dr, 0:nb], func=AF.Exp, scale=-1.0),
                    reads=[b_RC[half]], writes=[b_RC[half]])
                P.op("dve", lambda e, dr=dr, nr=nr, ob=ob, half=half: e.tensor_tensor(
                    TMP[half][nr, 0:nb], psum[nr, ob, 0:nb], RC[half][dr, 0:nb], ALU.mult),
                    reads=[b_O[half], b_RC[half]], writes=[b_TMP[half]])
                P.op("dve", lambda e, nr=nr, half=half: e.tensor_tensor(
                    MA[s][nr, c, 0:nb], TMP[half][nr, 0:nb], GA[s][nr, c, 0:nb], ALU.mult),
                    reads=[b_TMP[half], b_GA[s]], writes=[b_MA[s]])
                if debug:
                    P.dma("pool", lambda e, nr=nr: e.dma_start(out=mas_d[c, nr, tok0:tok0 + nb], in_=MA[s][nr, c, 0:nb]),
                          reads=[b_MA[s]], writes=[b_mas])

        t3count = [0]

        def make_oproj_groups(b):
            tl, nb, tok0 = blk[b]
            s = b % 2
            s3 = b % 3
            out = []
            for (ti, t0, tr) in tl:
                loc = t0 - tok0
                hs = t3count[0] % 2
                t3count[0] += 1
                for half in range(2):
                    def grp_fn(ti=ti, t0=t0, tr=tr, loc=loc, hs=hs, half=half):
                        PSO = psum[:, OPB, :]
                        P.op("dve", lambda e: e.tensor_tensor(
                            HN[hs][0:tr, half * 512:(half + 1) * 512], PSO[0:tr, :], HT3[hs][0:tr, half * 512:(half + 1) * 512],
                            ALU.add),
                            reads=[b_ps[OPB], b_HT3[hs]], writes=[b_HN[hs]])
                        if half == 0:
                            return
                        lo = max(t0, NMETA)
                        if not last:
                            P.dma("pool", lambda e: e.dma_start(out=hout[t0:t0 + tr, :], in_=HN[hs][0:tr, :]),
                                  reads=[b_HN[hs]], writes=[b_hout])
                            return
                        if final_norm:
                            P.op("act", lambda e: e.activation(
                                out=SQJ3[0:tr, :], in_=HN[hs][0:tr, :], func=AF.Square, accum_out=STAT[0:tr, hs:hs + 1]),
                                reads=[b_HN[hs]], writes=[b_SQJ3, b_st3[hs]])
                            P.op("act", lambda e: e.activation(
                                out=STAT[0:tr, 2 + hs:3 + hs], in_=STAT[0:tr, hs:hs + 1], func=AF.Ln, bias=EPSN[0:tr, :],
                                scale=1.0 / D),
                                reads=[b_st3[hs], b_stat], writes=[b_st3[hs]])
                            P.op("act", lambda e: e.activation(
                                out=STAT[0:tr, 4 + hs:5 + hs], in_=STAT[0:tr, 2 + hs:3 + hs], func=AF.Exp, scale=-0.5),
                                reads=[b_st3[hs]], writes=[b_st3[hs]])
                            P.op("dve", lambda e: e.scalar_tensor_tensor(
                                out=HN[hs][0:tr, :], in0=HN[hs][0:tr, :], scalar=STAT[0:tr, 4 + hs:5 + hs], in1=FG[0:tr, :],
                                op0=ALU.mult, op1=ALU.mult),
                                reads=[b_HN[hs], b_st3[hs], b_const], writes=[b_HN[hs]])
                        P.dma("pool", lambda e: e.dma_start(
                            out=out_d[lo - NMETA:t0 + tr - NMETA, :], in_=HN[hs][lo - t0:tr, :]),
                            reads=[b_HN[hs]], writes=[b_out])
                    def mm_fn(kc, t0=t0, tr=tr, loc=loc, hs=hs, half=half):
                        if half == 0 and kc == 0:
                            P.dma("sp", lambda e: e.dma_start(out=HT3[hs][0:tr, :], in_=hin[t0:t0 + tr, :]),
                                  reads=[b_hin], writes=[b_HT3[hs]])
                        lhsT = GFB[s3][:, kc, loc:loc + tr] if kc < 4 else MA[s][:, kc - 4, loc:loc + tr]
                        P.op("pe", lambda e: e.matmul(psum[0:tr, OPB, :], lhsT, WO[:, kc, half * 512:(half + 1) * 512],
                                                      start=(kc == 0), stop=(kc == KC - 1)),
                             reads=[b_GFB[s3], b_MA[s], b_WO], writes=[b_ps[OPB]])
                    for kc in range(KC):
                        if kc < KC - 1:
                            out.append(lambda kc=kc, mm_fn=mm_fn: mm_fn(kc))
                        else:
                            out.append(lambda kc=kc, mm_fn=mm_fn, grp_fn=grp_fn: (mm_fn(kc), grp_fn()))
            return out

        deferred = []
        emit_block_loads(0)
        emit_qk(0)
        for i in range(nsteps):
            b, c, t = steps[i]
            if c == 0 and t == 0 and b + 1 < NBLK:
                emit_block_loads(b + 1)
            if i + 1 < nsteps:
                emit_qk(i + 1)
            emit_exp_pv(i)
            if deferred:
                deferred.pop(0)()
            if t == NT - 1:
                emit_chunk_norm(b, c)
                if c == 3:
                    deferred.extend(make_oproj_groups(b))
        for fn in deferred:
            fn()

    for layer in range(n_layers):
        layer_body(layer)
    P.barrier()
    P.emit()
    return nc, stack


_PROG_CACHE = {}


def _host_inputs(x_b, meta_tokens, norm_g, w_in, w_fmix, b_fmix, q_norm_g, k_norm_g, w_out, final_g):
    c = _constants()
    f32 = np.float32
    ng = np.ascontiguousarray(norm_g.reshape(DEPTH, KC, 128).transpose(2, 0, 1).reshape(128, DEPTH * KC)).astype(f32)
    swap = np.arange(64).reshape(32, 2)[:, ::-1].reshape(64)
    d_of_p = np.arange(128) % 64
    qkg = np.empty((128, DEPTH, 4), f32)
    qkg[:, :, 0] = q_norm_g[:, d_of_p].T
    qkg[:, :, 1] = q_norm_g[:, swap[d_of_p]].T
    qkg[:, :, 2] = k_norm_g[:, d_of_p].T
    qkg[:, :, 3] = k_norm_g[:, swap[d_of_p]].T
    bfm = np.ascontiguousarray(b_fmix.reshape(DEPTH, 4, 128).transpose(2, 0, 1).reshape(128, DEPTH * 4)).astype(f32)
    fg = np.ascontiguousarray(np.broadcast_to(final_g.reshape(1, D), (128, D))).astype(f32)
    return {
        "x": np.ascontiguousarray(x_b, dtype=f32),
        "meta": np.ascontiguousarray(meta_tokens, dtype=f32),
        "w_in": np.ascontiguousarray(w_in.reshape(DEPTH, KC, 128, 2304), dtype=f32),
        "w_out": np.ascontiguousarray(w_out, dtype=f32),
        "w_fmix": np.ascontiguousarray(w_fmix.reshape(DEPTH, 4, 128, 64), dtype=f32),
        "ng": ng, "qkg": np.ascontiguousarray(qkg.reshape(128, DEPTH * 4)), "bfm": bfm, "fg": fg,
        "cosT": c["cosT"], "sinT": c["sinT"], "m1": c["m1"], "c3": c["c3"], "cs64": c["cs64"],
        "ident": c["ident"], "bones": c["bones"],
    }


def kernel(x, meta_tokens, norm_g, w_in, w_fmix, b_fmix, q_norm_g, k_norm_g, w_out, final_g):
    x = np.asarray(x)
    args = [np.asarray(a) for a in (meta_tokens, norm_g, w_in, w_fmix, b_fmix, q_norm_g, k_norm_g, w_out, final_g)]
    nb = x.shape[0]
    if "full" not in _PROG_CACHE:
        _PROG_CACHE["full"] = build_program(DEPTH)
    nc, _stack = _PROG_CACHE["full"]
    in_maps = [_host_inputs(x[i], *args) for i in range(nb)]
    res = run_bass_kernel_spmd(nc, in_maps, core_ids=list(range(nb)))
    out = np.stack([np.asarray(r["out"]) for r in res.results], axis=0)
    return out.astype(np.float32)
```
